# Optimizing a Trainium2 kernel written in Bass

```python
import math
import jax, jax.numpy as jnp
from jax import lax
import numpy as np

D_MODEL = 1024
BATCH = 4
SEQ = 4096
DEPTH = 1
DEC_BATCH = 1
DEC_SEQ = 16384
PAST_LEN = 128

CONV_WIDTH = D_MODEL // 2
CONV_HEADS = 8
CONV_K = 3
POOL_WINDOWS = (2, 4, 8, 16)
N_POOL_GROUPS = len(POOL_WINDOWS)
POOL_WIDTH = D_MODEL - CONV_WIDTH
POOL_GROUP = POOL_WIDTH // N_POOL_GROUPS
PROJ_WIDTH = 3 * CONV_WIDTH + POOL_WIDTH
D_FF = 4 * D_MODEL
LN_EPS = 1e-5
DEEPNORM_ALPHA = (2.0 * DEPTH) ** 0.25
DEEPNORM_BETA = (8.0 * DEPTH) ** -0.25

kernel_name = "hybrid_conv_pool_encoder"


def layer_norm(x, g, b):
    xf = x.astype(jnp.float32)
    mu = jnp.mean(xf, axis=-1, keepdims=True)
    var = jnp.mean(jnp.square(xf - mu), axis=-1, keepdims=True)
    y = (xf - mu) * lax.rsqrt(var + LN_EPS) * g.astype(jnp.float32) + b.astype(jnp.float32)
    return y.astype(x.dtype)


def centred_short_conv(u, w):
    up = jnp.pad(u, ((0, 0), (1, 1), (0, 0)))
    return up[:, :-2] * w[0] + up[:, 1:-1] * w[1] + up[:, 2:] * w[2]


def multiscale_pool(u, pool_w, pool_scale):
    bsz, s, _ = u.shape
    ug = u.reshape(bsz, s, N_POOL_GROUPS, POOL_GROUP)
    t = jnp.arange(s, dtype=jnp.int32)
    outs = []
    for i, w in enumerate(POOL_WINDOWS):
        xg = ug[:, :, i, :].astype(jnp.float32)
        cs = jnp.concatenate([jnp.zeros((bsz, 1, POOL_GROUP), jnp.float32),
                              jnp.cumsum(xg, axis=1)], axis=1)
        lo = jnp.clip(t - w // 2, 0, s)
        hi = jnp.clip(t + w - w // 2, 0, s)
        win_sum = jnp.take(cs, hi, axis=1) - jnp.take(cs, lo, axis=1)
        cnt = (hi - lo).astype(jnp.float32)[None, :, None]
        outs.append(win_sum / cnt - xg)
    pooled = jnp.stack(outs, axis=2).astype(u.dtype)
    mixed = jnp.einsum('bsgc,gcd->bsgd', pooled, pool_w)
    return mixed.reshape(bsz, s, POOL_WIDTH) * pool_scale


def encoder_layer(x, w_in, conv_w, pool_w, pool_scale, w_out, ln1_g, ln1_b, w_ff1, w_ff2, ln2_g, ln2_b):
    proj = jnp.einsum('bsd,dp->bsp', x, w_in)
    gate_b = proj[..., :CONV_WIDTH]
    gate_c = proj[..., CONV_WIDTH:2 * CONV_WIDTH]
    value = proj[..., 2 * CONV_WIDTH:3 * CONV_WIDTH]
    pool_in = proj[..., 3 * CONV_WIDTH:]
    y_conv = gate_b * centred_short_conv(gate_c * value, conv_w)
    y_pool = multiscale_pool(pool_in, pool_w, pool_scale)
    mix = jnp.einsum('bsc,cd->bsd', jnp.concatenate([y_conv, y_pool], axis=-1), w_out)
    x = layer_norm(DEEPNORM_ALPHA * x + mix, ln1_g, ln1_b)
    h = jnp.square(jax.nn.relu(jnp.einsum('bsd,df->bsf', x, w_ff1)))
    ff = jnp.einsum('bsf,fd->bsd', h, w_ff2)
    return layer_norm(DEEPNORM_ALPHA * x + ff, ln2_g, ln2_b)


def setup_inputs(seed: int = 0) -> dict:
    key = jax.random.key(seed)
    ks = jax.random.split(key, 16)
    f32 = jnp.float32
    x_prompt = jax.random.normal(ks[0], (BATCH, SEQ, D_MODEL), f32)
    x_sample = jax.random.normal(ks[1], (DEC_BATCH, DEC_SEQ, D_MODEL), f32)
    col_scale = jnp.concatenate([jnp.ones((2 * CONV_WIDTH,), f32),
                                 jnp.full((CONV_WIDTH + POOL_WIDTH,), DEEPNORM_BETA, f32)])
    w_in = jax.random.normal(ks[2], (DEPTH, D_MODEL, PROJ_WIDTH), f32) * (D_MODEL ** -0.5) * col_scale
    conv_w = jax.random.normal(ks[3], (DEPTH, CONV_K, CONV_WIDTH), f32) * (CONV_K ** -0.5)
    pool_w = jax.random.normal(ks[4], (DEPTH, N_POOL_GROUPS, POOL_GROUP, POOL_GROUP), f32) * (POOL_GROUP ** -0.5)
    pool_scale = 1.0 + 0.1 * jax.random.normal(ks[5], (DEPTH, POOL_WIDTH), f32)
    w_out = jax.random.normal(ks[6], (DEPTH, D_MODEL, D_MODEL), f32) * (D_MODEL ** -0.5) * DEEPNORM_BETA
    ln1_g = 1.0 + 0.02 * jax.random.normal(ks[7], (DEPTH, D_MODEL), f32)
    ln1_b = 0.02 * jax.random.normal(ks[8], (DEPTH, D_MODEL), f32)
    w_ff1 = jax.random.normal(ks[9], (DEPTH, D_MODEL, D_FF), f32) * (D_MODEL ** -0.5) * DEEPNORM_BETA
    w_ff2 = jax.random.normal(ks[10], (DEPTH, D_FF, D_MODEL), f32) * (D_FF ** -0.5) * DEEPNORM_BETA
    ln2_g = 1.0 + 0.02 * jax.random.normal(ks[11], (DEPTH, D_MODEL), f32)
    ln2_b = 0.02 * jax.random.normal(ks[12], (DEPTH, D_MODEL), f32)
    return {"x_prompt": x_prompt, "x_sample": x_sample, "w_in": w_in, "conv_w": conv_w,
            "pool_w": pool_w, "pool_scale": pool_scale, "w_out": w_out, "ln1_g": ln1_g,
            "ln1_b": ln1_b, "w_ff1": w_ff1, "w_ff2": w_ff2, "ln2_g": ln2_g, "ln2_b": ln2_b}


def reference(x_prompt, x_sample, w_in, conv_w, pool_w, pool_scale, w_out, ln1_g, ln1_b,
              w_ff1, w_ff2, ln2_g, ln2_b):
    y_prompt = x_prompt
    y_sample = x_sample
    for l in range(DEPTH):
        p = (w_in[l], conv_w[l], pool_w[l], pool_scale[l], w_out[l], ln1_g[l], ln1_b[l],
             w_ff1[l], w_ff2[l], ln2_g[l], ln2_b[l])
        y_prompt = encoder_layer(y_prompt, *p)
        y_sample = encoder_layer(y_sample, *p)
    return (y_prompt, y_sample)
```

```python
from contextlib import ExitStack

import numpy as np
import concourse.bass as bass
import concourse.mybir as mybir
from concourse.bass_utils import run_bass_kernel_spmd

F32 = mybir.dt.float32
BF16 = mybir.dt.bfloat16
AF = mybir.ActivationFunctionType
ALU = mybir.AluOpType

D = 1024
DFF = 4096
NTOK = 4096
T = 512
NT = NTOK // T
NS = T // 128
TE = T + 16
XE_ROWS = NTOK + 16
ALPHA = 2.0 ** 0.25
EPS = 1e-5
POOL_W = (2, 4, 8, 16)
N_CORES = 8


class Buf:
    __slots__ = ("name", "w", "r", "const")

    def __init__(self, name, const=False):
        self.name = name
        self.w = None
        self.r = []
        self.const = const


class Eng:
    def __init__(self, name, sem, same_engine_sync=True):
        self.name = name
        self.sem = sem
        self.cnt = 0
        self.seen = {}
        self.ops = []
        self.same_engine_sync = same_engine_sync

    def _wait_for(self, toks):
        need = {}
        for tok in toks:
            if tok is None:
                continue
            sem, val = tok
            if sem is self.sem and not self.same_engine_sync:
                continue
            if self.seen.get(id(sem), 0) >= val:
                continue
            if need.get(id(sem), (None, 0))[1] < val:
                need[id(sem)] = (sem, val)
        for sem, val in need.values():
            self.ops.append(("wait", sem, val))
            self.seen[id(sem)] = val

    @staticmethod
    def _deps(reads, writes):
        toks = []
        for b in reads:
            toks.append(b.w)
        for b in writes:
            toks.append(b.w)
            toks.extend(b.r)
        return toks

    @staticmethod
    def _record(tok, reads, writes):
        for b in reads:
            if not b.const:
                b.r.append(tok)
        for b in writes:
            b.w = tok
            b.r = []

    def emit(self, fn, reads=(), writes=(), signal=True, excl=()):
        self._wait_for(self._deps(reads, writes))
        own = self.sem
        self._wait_for([t for t in self._deps((), excl) if t is not None and t[0] is not own])
        writes = list(writes) + list(excl)
        if signal:
            self.cnt += 1
            tok = (self.sem, self.cnt)
        else:
            tok = (self.sem, self.cnt + 1)
        self.ops.append(("ins", fn, self.sem if signal else None, 1))
        self._record(tok, reads, writes)

    def dma(self, fn, sem_state, reads=(), writes=()):
        self._wait_for(self._deps(reads, writes))
        sem_state[1] += 16
        tok = (sem_state[0], sem_state[1])
        self.ops.append(("ins", fn, sem_state[0], 16))
        self._record(tok, reads, writes)
        return tok

    def dma_multi(self, fns, sem_state, reads=(), writes=()):
        self._wait_for(self._deps(reads, writes))
        for fn in fns:
            sem_state[1] += 16
            self.ops.append(("ins", fn, sem_state[0], 16))
        tok = (sem_state[0], sem_state[1])
        self._record(tok, reads, writes)
        return tok

    def wait_tok(self, tok):
        self._wait_for([tok])

    def replay(self, h):
        for op in self.ops:
            if op[0] == "wait":
                h.wait_ge(op[1], op[2])
            else:
                ins = op[1](h)
                if op[2] is not None:
                    ins.then_inc(op[2], op[3])


def alias_barrier(new_bufs, old_bufs):
    toks = []
    for b in old_bufs:
        if b.w is not None:
            toks.append(b.w)
        toks.extend(b.r)
    for b in new_bufs:
        b.r = list(b.r) + toks


def build_program():
    nc = bass.Bass("TRN2", target_bir_lowering=False)

    def din(name, shape, dt=F32):
        return nc.dram_tensor(name, list(shape), dt, kind="ExternalInput").ap()

    xe = din("xe", [XE_ROWS, D])
    xet = din("xet", [NT, 128, 8, TE])
    w_in = din("w_in", [D, 2048])
    w_out = din("w_out", [D, D])
    w_ff1 = din("w_ff1", [D, DFF])
    w_ff2 = din("w_ff2", [DFF, D])
    cw_d = din("cw", [128, 12])
    pw_d = din("pw", [4, 128, 128])
    psc_d = din("psc", [128, 4])
    ecnt_d = din("ecnt", [128, 64])
    g1_d = din("g1r", [128, D])
    b1_d = din("b1r", [128, D])
    g1c_d = din("g1c", [128, 8])
    b1c_d = din("b1c", [128, 8])
    g2_d = din("g2r", [128, D])
    b2_d = din("b2r", [128, D])
    y = nc.dram_tensor("y", [NTOK, D], F32, kind="ExternalOutput").ap()

    scr_ff1 = nc.dram_tensor("scr_ff1", [8, 128, 8, 512], BF16).ap()
    scr_ff2 = nc.dram_tensor("scr_ff2", [8, 128, 4, 1024], BF16).ap()

    with ExitStack() as es:
        E = es.enter_context

        def sb(name, shape, dt=F32):
            return E(nc.sbuf_tensor(name, list(shape), dt))

        ident = sb("ident", [128, 128])
        cw = sb("cw_s", [128, 12])
        psc = sb("psc_s", [128, 4])
        ecnt = sb("ecnt_s", [128, 64])
        pw = sb("pw_s", [128, 4, 128], BF16)
        g1 = sb("g1_s", [128, D])
        b1 = sb("b1_s", [128, D])
        g1c = sb("g1c_s", [128, 8])
        b1c = sb("b1c_s", [128, 8])
        g2 = sb("g2_s", [128, D])
        b2 = sb("b2_s", [128, D])
        win_all = sb("win_all", [128, 8, 2048], BF16)
        wout = sb("wout", [128, 8, 1024], BF16)
        wff1 = [sb(f"wff1_{i}", [128, 8, 512], BF16) for i in range(3)]
        wff2 = [sb(f"wff2_{i}", [128, 4, 1024], BF16) for i in range(2)]
        xT = sb("xT", [128, 8, TE], BF16)
        xr = [sb(f"xr{i}", [128, D]) for i in range(2)]
        x1 = sb("x1", [128, NS, D])
        x1T = sb("x1T", [128, 8, T], BF16)
        rtmp = [sb(f"rtmp{i}", [128, T]) for i in range(2)]
        vbuf = [sb(f"vbuf{i}", [128, D]) for i in range(4)]
        st = [sb(f"st{i}", [128, 12]) for i in range(2)]
        mv = [sb(f"mv{i}", [128, 2]) for i in range(2)]
        sd = [sb(f"sd{i}", [128, 1]) for i in range(2)]
        rstd = [sb(f"rstd{i}", [128, 1]) for i in range(2)]
        tmp8 = sb("tmp8", [128, 8])
        nhalf = sb("nhalf", [128, 1])
        R = sb("R", [128, 8448])
        hT = R[:, 0:8192].bitcast(BF16).rearrange("p (f t) -> p f t", t=T)
        vS = R[:, 2112:2640]
        gcv = R[:, 2640:3168]
        acc = R[:, 3168:3696]
        pS = R[:, 3696:4224]
        Aa = R[:, 4224:4752]
        Ab = R[:, 4752:5280]
        pooled = R[:, 5280:6304].bitcast(BF16).rearrange("p (g t) -> p g t", t=T)
        ycat = R[:, 6304:8352].bitcast(BF16).rearrange("p (k t) -> p k t", t=T)
        bank = [E(nc.psum_tensor(f"bank{i}", [128, 512], F32)) for i in range(8)]

        def sem(name):
            return E(nc.semaphore(name))

        pe = Eng("pe", sem("s_pe"), same_engine_sync=False)
        act = Eng("act", sem("s_act"))
        dve = Eng("dve", sem("s_dve"))
        pool = Eng("pool", sem("s_pool"))
        sp = Eng("sp", sem("s_sp"), same_engine_sync=False)

        B_ident = Buf("ident")
        B_consts = Buf("consts")
        B_pw = Buf("pw")
        B_win = [Buf("win_lo"), Buf("win_hi")]
        B_win_lo = [Buf("win_lo_a"), Buf("win_lo_b")]
        B_stage = Buf("w_in_stage")
        B_wout = Buf("wout")
        B_wff1 = [Buf(f"wff1_{i}") for i in range(3)]
        B_wff2 = [Buf(f"wff2_{i}") for i in range(2)]
        B_xr = [Buf(f"xr{i}") for i in range(2)]
        B_x1 = [[Buf(f"x1_{s}_{n}") for n in range(2)] for s in range(NS)]
        B_x1T = [[Buf(f"x1T_{s}_{k}") for k in range(8)] for s in range(NS)]
        B_rtmp = [Buf(f"rtmp{i}") for i in range(4)]
        B_vbuf = [[Buf(f"vbuf{i}_{n}") for n in range(2)] for i in range(4)]
        B_st = [[Buf(f"st{i}_{n}") for n in range(2)] for i in range(2)]
        B_mv = [Buf(f"mv{i}") for i in range(2)]
        B_sd = [Buf(f"sd{i}") for i in range(2)]
        B_rstd = [Buf(f"rstd{i}") for i in range(2)]
        B_tmp8 = Buf("tmp8")
        B_h = [[Buf(f"h{f}_{hf}") for hf in range(2)] for f in range(32)]
        B_h_all = [b for pair in B_h for b in pair]
        B_xT = Buf("xT")
        B_vS = [Buf("vS0"), Buf("vS1")]
        B_gcv = [Buf("gcv0"), Buf("gcv1")]
        B_pS = [Buf("pS0"), Buf("pS1")]
        B_acc, B_Aa, B_Ab = Buf("acc"), Buf("Aa"), Buf("Ab")
        B_pooled = [Buf(f"pooled{g}") for g in range(4)]
        B_ycat = [Buf(f"ycat{k}") for k in range(8)]
        B_bank = [Buf(f"bank{i}") for i in range(8)]
        B_scr_ff1 = [Buf(f"scr_ff1_{g}") for g in range(8)]
        B_scr_ff2 = [Buf(f"scr_ff2_{g}") for g in range(8)]

        def mix_bufs():
            out = B_vS + B_gcv + B_pS + [B_acc, B_Aa, B_Ab] + B_pooled + B_ycat
            return out

        def dsem(name):
            return [sem(name), 0]

        S_const = dsem("d_const")
        S_pw = dsem("d_pw")
        S_scr_ff1 = [dsem(f"d_sff1_{g}") for g in range(8)]
        S_scr_ff2 = [dsem(f"d_sff2_{g}") for g in range(8)]
        S_win = [dsem("d_win_lo"), dsem("d_win_hi")]
        S_wout = dsem("d_wout")
        S_wff1 = [dsem(f"d_wff1_{i}") for i in range(3)]
        S_wff2 = [dsem(f"d_wff2_{i}") for i in range(2)]
        S_xT = dsem("d_xT")
        S_xr = [dsem(f"d_xr{i}") for i in range(2)]
        S_y = [dsem(f"d_y{i}") for i in range(4)]

        pool.emit(lambda h: h.memset(ident[:], 0.0), writes=[B_ident])
        pool.emit(lambda h: h.affine_select(out=ident[:], in_=ident[:], pattern=[[-1, 128]],
                                            compare_op=ALU.not_equal, fill=1.0, base=0,
                                            channel_multiplier=1), writes=[B_ident])
        pool.emit(lambda h: h.memset(nhalf[:], -0.5), writes=[B_ident])
        pool.dma(lambda h: h.dma_start(out=pw[:], in_=pw_d.rearrange("g c d -> c g d")), S_pw, writes=[B_pw])
        def cast_chain():
            hist = []

            def one(fn, sem_state, buf):
                if len(hist) >= 2:
                    pool.wait_tok(hist[-2])
                hist.append(pool.dma(fn, sem_state, writes=[buf]))

            one(lambda h: h.dma_start(out=wout[:], in_=w_out.rearrange("(k p) d -> p k d", p=128)), S_wout, B_wout)
            B_wout.const = True

        B_win_hi = [Buf("win_hi_a")] + [Buf(f"win_hi_b{kk}") for kk in range(4)]
        x1_all = [b for ss in range(NS) for b in B_x1[ss]]
        sp.dma(lambda h: h.dma_start(out=x1[:, 0:4, :],
                                     in_=w_in[0:512, 1024:2048].rearrange("(k p) c -> p k c", p=128)),
               S_win[1], writes=x1_all)
        for kk in range(4):
            sp.dma(lambda h, kk=kk: h.dma_start(out=vbuf[kk][:], in_=w_in[(4 + kk) * 128:(5 + kk) * 128, 1024:2048]),
                   S_y[kk], writes=B_vbuf[kk])
        act.emit(lambda h: h.activation(out=win_all[:, 0:4, 1024:2048], in_=x1[:, 0:4, :], func=AF.Copy),
                 reads=x1_all, writes=[B_win_hi[0]])
        for kk in range(4):
            dve.emit(lambda h, kk=kk: h.tensor_copy(win_all[:, 4 + kk, 1024:2048], vbuf[kk][:]),
                     reads=B_vbuf[kk], writes=[B_win_hi[1 + kk]])
        for b_ in B_win_hi:
            b_.const = True

        stage = R[:, 0:8192].rearrange("p (k c) -> p k c", c=1024)
        sp.dma(lambda h: h.dma_start(out=stage, in_=w_in[:, 0:1024].rearrange("(k p) c -> p k c", p=128)),
               S_win[0], writes=[B_stage])
        act.emit(lambda h: h.activation(out=win_all[:, 0:4, 0:1024], in_=stage[:, 0:4, :], func=AF.Copy),
                 reads=[B_stage], writes=[B_win_lo[0]])
        dve.emit(lambda h: h.tensor_copy(win_all[:, 4:8, 0:1024], stage[:, 4:8, :]),
                 reads=[B_stage], writes=[B_win_lo[1]])
        B_win_lo[0].const = True
        B_win_lo[1].const = True
        alias_barrier(mix_bufs() + B_h_all, [B_stage])

        consts = ((cw, cw_d), (psc, psc_d), (ecnt, ecnt_d), (g1c, g1c_d), (b1c, b1c_d), (g1, g1_d), (b1, b1_d),
                  (g2, g2_d), (b2, b2_d))
        sp.dma_multi([(lambda h, dst=dst, src=src: h.dma_start(out=dst[:], in_=src)) for dst, src in consts],
                     S_const, writes=[B_consts])
        B_consts.const = True
        B_ident.const = True
        B_pw.const = True

        xr_ctr = [0]
        win_ctr = [0]
        wff1_ctr = [0]
        wff2_ctr = [0]
        ln_ctr = [0]

        def load_xT(i):
            pool.dma(lambda h: h.dma_start(out=xT[:], in_=xet[i]), S_xT, writes=[B_xT])

        def load_xr(i, s):
            slot = xr_ctr[0] % 2
            xr_ctr[0] += 1
            r0 = 8 + T * i + 128 * s
            sp.dma(lambda h: h.dma_start(out=xr[slot][:], in_=xe[r0:r0 + 128, :]), S_xr[slot], writes=[B_xr[slot]])
            return slot

        first_pass = [True]

        def load_wff1(g):
            slot = wff1_ctr[0] % 3
            wff1_ctr[0] += 1
            if first_pass[0]:
                pool.dma(lambda h: h.dma_start(
                    out=wff1[slot][:], in_=w_ff1[:, g * 512:(g + 1) * 512].rearrange("(k p) c -> p k c", p=128)),
                    S_wff1[slot], writes=[B_wff1[slot]])
                sp.dma(lambda h: h.dma_start(out=scr_ff1[g], in_=wff1[slot][:]), S_scr_ff1[g],
                       reads=[B_wff1[slot]], writes=[B_scr_ff1[g]])
            else:
                assert B_scr_ff1[g].w is not None, "w_ff1 scratch group read before it was written"
                sp.dma(lambda h: h.dma_start(out=wff1[slot][:], in_=scr_ff1[g]), S_wff1[slot],
                       reads=[B_scr_ff1[g]], writes=[B_wff1[slot]])
            return slot

        def load_wff2(g):
            slot = wff2_ctr[0] % 2
            wff2_ctr[0] += 1
            if first_pass[0]:
                pool.dma(lambda h: h.dma_start(
                    out=wff2[slot][:], in_=w_ff2[g * 512:(g + 1) * 512, :].rearrange("(f p) d -> p f d", p=128)),
                    S_wff2[slot], writes=[B_wff2[slot]])
                sp.dma(lambda h: h.dma_start(out=scr_ff2[g], in_=wff2[slot][:]), S_scr_ff2[g],
                       reads=[B_wff2[slot]], writes=[B_scr_ff2[g]])
            else:
                assert B_scr_ff2[g].w is not None, "w_ff2 scratch group read before it was written"
                sp.dma(lambda h: h.dma_start(out=wff2[slot][:], in_=scr_ff2[g]), S_wff2[slot],
                       reads=[B_scr_ff2[g]], writes=[B_wff2[slot]])
            return slot

        def load_wff2_t0(g, slot3):
            wt, wb, ws = slot3
            pool.dma(lambda h: h.dma_start(
                out=wt, in_=w_ff2[g * 512:(g + 1) * 512, :].rearrange("(f p) d -> p f d", p=128)), ws, writes=[wb])
            sp.dma(lambda h: h.dma_start(out=scr_ff2[g], in_=wt), S_scr_ff2[g], reads=[wb], writes=[B_scr_ff2[g]])

        def ln_evac(src_banks, resid_ap, resid_bufs, v, vbufs):
            for n in range(2):
                bi = src_banks[n]
                dve.emit(lambda h, n=n, bi=bi: h.scalar_tensor_tensor(
                    out=v[:, n * 512:(n + 1) * 512], in0=resid_ap[:, n * 512:(n + 1) * 512], scalar=ALPHA,
                    in1=bank[bi][:], op0=ALU.mult, op1=ALU.add),
                    reads=resid_bufs, writes=[vbufs[n]], excl=[B_bank[bi]])

        def ln_stats(v, vbufs, use_pool=True):
            p = ln_ctr[0] % 2
            ln_ctr[0] += 1
            for n in range(2):
                dve.emit(lambda h, n=n: h.bn_stats(out=st[p][:, n * 6:(n + 1) * 6], in_=v[:, n * 512:(n + 1) * 512]),
                         reads=[vbufs[n]], writes=[B_st[p][n]])
            dve.emit(lambda h: h.bn_aggr(out=mv[p][:], in_=st[p][:]), reads=B_st[p], writes=[B_mv[p]])
            if not use_pool:
                act.emit(lambda h: h.activation(out=sd[p][:], in_=mv[p][:, 1:2], func=AF.Sqrt, bias=EPS, scale=1.0),
                         reads=[B_mv[p]], writes=[B_sd[p]])
                dve.emit(lambda h: h.reciprocal(out=rstd[p][:], in_=sd[p][:]), reads=[B_sd[p]], writes=[B_rstd[p]])
                return p
            pool.emit(lambda h: h.tensor_scalar(out=sd[p][:], in0=mv[p][:, 1:2], scalar1=EPS, scalar2=None,
                                                op0=ALU.add),
                      reads=[B_mv[p]], writes=[B_sd[p]])
            pool.emit(lambda h: h.tensor_tensor(out=rstd[p][:], in0=sd[p][:], in1=nhalf[:], op=ALU.pow),
                      reads=[B_sd[p], B_ident], writes=[B_rstd[p]])
            return p

        def ln_norm(v, vbufs, p):
            dve.emit(lambda h: h.tensor_scalar(out=v, in0=v, scalar1=mv[p][:, 0:1], scalar2=rstd[p][:, 0:1],
                                               op0=ALU.subtract, op1=ALU.mult),
                     reads=[B_mv[p], B_rstd[p]], writes=vbufs)

        def ln_affine(v, vbufs, p, gam, bet):
            dve.emit(lambda h: h.scalar_tensor_tensor(out=v, in0=v, scalar=mv[p][:, 0:1], in1=gam[:],
                                                      op0=ALU.subtract, op1=ALU.mult),
                     reads=[B_mv[p], B_consts], writes=vbufs)
            dve.emit(lambda h: h.scalar_tensor_tensor(out=v, in0=v, scalar=rstd[p][:, 0:1], in1=bet[:],
                                                      op0=ALU.mult, op1=ALU.add),
                     reads=[B_rstd[p], B_consts], writes=vbufs)

        load_xT(0)
        cast_chain()
        wff1_slots = {}
        wff2_slots = {}
        store_toks = []
        deferred_ln2 = []

        for i in range(NT):
            first_pass[0] = (i == 0)
            xT_all = [B_xT]

            xr_slots = {0: load_xr(i, 0), 1: load_xr(i, 1)}
            for g in range(3):
                wff1_slots[("A", g)] = load_wff1(g)

            main_rr = [0]
            ext_rr = [0]

            HALF = TE // 2

            def proj_chunk(cb, jj, with_ext, main_lo):
                lhs = lambda k: win_all[:, k, cb * 512 + jj * 128:cb * 512 + (jj + 1) * 128]
                if not with_ext:
                    mb = main_rr[0] % 7
                    main_rr[0] += 1
                    for k in range(8):
                        pe.emit(lambda h, k=k: h.matmul(
                            bank[mb][:], lhsT=lhs(k), rhs=xT[:, k, main_lo:main_lo + 512],
                            start=(k == 0), stop=(k == 7)),
                            reads=(B_win_lo if cb < 2 else B_win_hi) + xT_all, writes=[B_bank[mb]], signal=(k == 7))
                    return mb, None
                b0 = main_rr[0] % 7
                b1 = (main_rr[0] + 1) % 7
                main_rr[0] += 2
                for k in range(8):
                    for bb, c0 in ((b0, 0), (b1, HALF)):
                        pe.emit(lambda h, k=k, bb=bb, c0=c0: h.matmul(
                            bank[bb][:, 0:HALF], lhsT=lhs(k), rhs=xT[:, k, c0:c0 + HALF],
                            start=(k == 0), stop=(k == 7)),
                            reads=(B_win_lo if cb < 2 else B_win_hi) + xT_all, writes=[B_bank[bb]], signal=(k == 7 and bb == b1))
                return b0, b1

            pending_poolmix = []

            def do_poolmix(g):
                pe.emit(lambda h: h.matmul(bank[7][:], lhsT=pw[:, g, :], rhs=pooled[:, g, :], start=True, stop=True),
                        reads=[B_pw, B_pooled[g]], writes=[B_bank[7]])
                act.emit(lambda h: h.activation(out=ycat[:, 4 + g, :], in_=bank[7][:], func=AF.Identity,
                                                scale=psc[:, g:g + 1]),
                         reads=[B_consts], writes=[B_ycat[4 + g]], excl=[B_bank[7]])

            sum_eng = dve if i == 0 else pool
            for j in range(4):
                pj = 3 - j
                mb_v, eb_v = proj_chunk(2, j, True, 0)
                act.emit(lambda h, mb=mb_v: h.activation(out=vS[:, 0:HALF], in_=bank[mb][:, 0:HALF], func=AF.Copy),
                         writes=[B_vS[0]], excl=[B_bank[mb_v]])
                act.emit(lambda h, eb=eb_v: h.activation(out=vS[:, HALF:TE], in_=bank[eb][:, 0:HALF], func=AF.Copy),
                         writes=[B_vS[1]], excl=[B_bank[eb_v]])
                mb_c, eb_c = proj_chunk(1, j, True, 0)
                dve.emit(lambda h, mb=mb_c: h.tensor_tensor(out=gcv[:, 0:HALF], in0=bank[mb][:, 0:HALF],
                                                            in1=vS[:, 0:HALF], op=ALU.mult),
                         reads=[B_vS[0]], writes=[B_gcv[0]], excl=[B_bank[mb_c]])
                dve.emit(lambda h, eb=eb_c: h.tensor_tensor(out=gcv[:, HALF:TE], in0=bank[eb][:, 0:HALF],
                                                            in1=vS[:, HALF:TE], op=ALU.mult),
                         reads=[B_vS[1]], writes=[B_gcv[1]], excl=[B_bank[eb_c]])
                act.emit(lambda h, j=j: h.activation(out=acc[:, 0:512], in_=gcv[:, 8:520], func=AF.Identity,
                                                     scale=cw[:, 3 * j + 1:3 * j + 2]),
                         reads=B_gcv + [B_consts], writes=[B_acc])
                dve.emit(lambda h, j=j: h.scalar_tensor_tensor(out=acc[:, 0:512], in0=gcv[:, 7:519],
                                                               scalar=cw[:, 3 * j:3 * j + 1], in1=acc[:, 0:512],
                                                               op0=ALU.mult, op1=ALU.add),
                         reads=B_gcv + [B_consts], writes=[B_acc])
                dve.emit(lambda h, j=j: h.scalar_tensor_tensor(out=acc[:, 0:512], in0=gcv[:, 9:521],
                                                               scalar=cw[:, 3 * j + 2:3 * j + 3], in1=acc[:, 0:512],
                                                               op0=ALU.mult, op1=ALU.add),
                         reads=B_gcv + [B_consts], writes=[B_acc])
                mb_b, _ = proj_chunk(0, j, False, 8)
                dve.emit(lambda h, mb=mb_b, j=j: h.tensor_tensor(out=ycat[:, j, :], in0=bank[mb][:], in1=acc[:, 0:512],
                                                                op=ALU.mult),
                         reads=[B_acc], writes=[B_ycat[j]], excl=[B_bank[mb_b]])
                mb_p, eb_p = proj_chunk(3, pj, True, 0)
                act.emit(lambda h, mb=mb_p: h.activation(out=pS[:, 0:HALF], in_=bank[mb][:, 0:HALF], func=AF.Copy),
                         writes=[B_pS[0]], excl=[B_bank[mb_p]])
                act.emit(lambda h, eb=eb_p: h.activation(out=pS[:, HALF:TE], in_=bank[eb][:, 0:HALF], func=AF.Copy),
                         writes=[B_pS[1]], excl=[B_bank[eb_p]])
                w = POOL_W[pj]
                cur, cur_b, L = pS, B_pS, TE
                pp = [(Aa, [B_Aa]), (Ab, [B_Ab])]
                m = 1
                step = 0
                while m < w:
                    dst, dst_b = pp[step % 2]
                    L2 = L - m
                    sum_eng.emit(lambda h, dst=dst, cur=cur, L2=L2, m=m: h.tensor_tensor(
                        out=dst[:, 0:L2], in0=cur[:, 0:L2], in1=cur[:, m:m + L2], op=ALU.add),
                        reads=cur_b, writes=dst_b)
                    cur, cur_b, L = dst, dst_b, L2
                    m *= 2
                    step += 1
                off = 8 - w // 2
                dve.emit(lambda h, cur=cur, off=off, pj=pj, w=w: h.scalar_tensor_tensor(
                    out=pooled[:, pj, :], in0=cur[:, off:off + 512], scalar=1.0 / w, in1=pS[:, 8:520],
                    op0=ALU.mult, op1=ALU.subtract),
                    reads=cur_b + B_pS, writes=[B_pooled[pj]])
                if i == 0 or i == NT - 1:
                    q0 = 0 if i == 0 else 504
                    e0 = 16 * pj + (0 if i == 0 else 8)
                    dve.emit(lambda h, cur=cur, off=off, q0=q0, e0=e0: h.tensor_tensor(
                        out=tmp8[:], in0=cur[:, off + q0:off + q0 + 8], in1=ecnt[:, e0:e0 + 8], op=ALU.mult),
                        reads=cur_b + [B_consts], writes=[B_tmp8])
                    dve.emit(lambda h, q0=q0, pj=pj: h.tensor_tensor(
                        out=pooled[:, pj, q0:q0 + 8], in0=tmp8[:], in1=pS[:, 8 + q0:16 + q0], op=ALU.subtract),
                        reads=[B_tmp8] + B_pS, writes=[B_pooled[pj]])
                if pending_poolmix:
                    do_poolmix(pending_poolmix.pop(0))
                pending_poolmix.append(pj)
            while len(pending_poolmix) > 1:
                do_poolmix(pending_poolmix.pop(0))
            if i + 1 < NT:
                load_xT(i + 1)

            if i > 0:
                for g in range(2):
                    wff2_slots[g] = load_wff2(g)

            korder = [0, 7, 1, 6, 2, 5, 3, 4]

            def wo_mm(s, n, idx):
                k = korder[idx]
                bi = 2 * s + n
                pe.emit(lambda h: h.matmul(
                    bank[bi][:], lhsT=ycat[:, k, 128 * s:128 * s + 128],
                    rhs=wout[:, k, 512 * n:512 * n + 512], start=(idx == 0), stop=(idx == 7)),
                    reads=[B_ycat[k], B_wout], writes=[B_bank[bi]], signal=(idx == 7))

            for idx in range(7):
                for n in range(2):
                    wo_mm(0, n, idx)
            do_poolmix(pending_poolmix.pop(0))
            for n in range(2):
                wo_mm(0, n, 7)
            for idx in range(8):
                for n in range(2):
                    wo_mm(1, n, idx)

            def wo_pair1(idx_range):
                for idx in idx_range:
                    for s in (2, 3):
                        for n in range(2):
                            wo_mm(s, n, idx)

            def ln1(s):
                slot = xr_slots.pop(s)
                v = x1[:, s, :]
                ln_evac((2 * s, 2 * s + 1), xr[slot], [B_xr[slot]], v, B_x1[s])
                if s + 2 < NS:
                    xr_slots[s + 2] = load_xr(i, s + 2)
                p = ln_stats(v, B_x1[s], use_pool=(i > 0))
                ln_norm(v, B_x1[s], p)

            def tr1(s):
                for hh in range(2):
                    bi = (2 * s + hh) % 8
                    for kk in range(4):
                        k = 4 * hh + kk
                        pe.emit(lambda h, bi=bi, kk=kk, k=k: h.transpose(
                            bank[bi][:, kk * 128:(kk + 1) * 128], x1[:, s, k * 128:(k + 1) * 128], ident[:]),
                            reads=B_x1[s] + [B_ident], writes=[B_bank[bi]], signal=(kk == 3))
                    for kk in range(4):
                        k = 4 * hh + kk
                        if hh == 0:
                            act.emit(lambda h, bi=bi, kk=kk, k=k: h.activation(
                                out=x1T[:, k, 128 * s:128 * s + 128], in_=bank[bi][:, kk * 128:(kk + 1) * 128],
                                func=AF.Identity, scale=g1c[:, k:k + 1], bias=b1c[:, k:k + 1]),
                                reads=[B_consts], writes=[B_x1T[s][k]], excl=[B_bank[bi]])
                        else:
                            dve.emit(lambda h, bi=bi, kk=kk, k=k: h.tensor_scalar(
                                out=x1T[:, k, 128 * s:128 * s + 128], in0=bank[bi][:, kk * 128:(kk + 1) * 128],
                                scalar1=g1c[:, k:k + 1], scalar2=b1c[:, k:k + 1], op0=ALU.mult, op1=ALU.add),
                                reads=[B_consts], writes=[B_x1T[s][k]], excl=[B_bank[bi]])

            def ff1_group(hf, g, slot):
                x1T_half = [B_x1T[s][k] for s in (2 * hf, 2 * hf + 1) for k in range(8)]
                c0 = 256 * hf
                for fl in range(4):
                    f = 4 * g + fl
                    bi = f % (4 if hf == 0 else 8)
                    rp = f % 4
                    rt = rtmp[rp // 2][:, 256 * (rp % 2):256 * (rp % 2) + 256]
                    for k in range(8):
                        pe.emit(lambda h, bi=bi, k=k, fl=fl: h.matmul(
                            bank[bi][:, 0:256], lhsT=wff1[slot][:, k, fl * 128:(fl + 1) * 128],
                            rhs=x1T[:, k, c0:c0 + 256], start=(k == 0), stop=(k == 7)),
                            reads=[B_wff1[slot]] + x1T_half, writes=[B_bank[bi]], signal=(k == 7))
                    act.emit(lambda h, bi=bi, rt=rt: h.activation(out=rt, in_=bank[bi][:, 0:256], func=AF.Relu),
                             writes=[B_rtmp[rp]], excl=[B_bank[bi]])
                    dve.emit(lambda h, rt=rt, f=f: h.tensor_tensor(out=hT[:, f, c0:c0 + 256], in0=rt, in1=rt,
                                                                   op=ALU.mult),
                             reads=[B_rtmp[rp]], writes=[B_h[f][hf]])
                    if hf == 0 and f >= 12 and deferred_ln2:
                        deferred_ln2.pop(0)()

            ln1(0)
            ln1(1)
            wo_pair1(range(0, 2))
            tr1(0)
            wo_pair1(range(2, 6))
            tr1(1)
            wo_pair1(range(6, 8))
            alias_barrier(B_h_all, mix_bufs())
            ln1(2)
            ln1(3)
            if i > 0:
                first = [wff1_slots.pop(("A", g)) for g in range(3)]
                ff1_group(0, 0, first[0])
                ff1_group(0, 1, first[1])
                tr1(2)
                tr1(3)
                ff1_group(0, 2, first[2])
                for g in range(3):
                    ff1_group(1, g, first[g])
                    wff1_slots[("A", g + 3)] = load_wff1(g + 3)
                for g in range(3, 8):
                    slot = wff1_slots.pop(("A", g))
                    ff1_group(0, g, slot)
                    ff1_group(1, g, slot)
                    if g + 3 < 8:
                        wff1_slots[("A", g + 3)] = load_wff1(g + 3)
                while deferred_ln2:
                    deferred_ln2.pop(0)()
            else:
                for g in range(2):
                    wff2_slots[g] = load_wff2(g)
                tr1(2)
                tr1(3)
                for g in range(8):
                    slot = wff1_slots.pop(("A", g))
                    ff1_group(0, g, slot)
                    ff1_group(1, g, slot)
                    if g + 3 < 8:
                        wff1_slots[("A", g + 3)] = load_wff1(g + 3)
                t0_ring = {0: (wff2[0][:], B_wff2[0], S_wff2[0]), 1: (wff2[1][:], B_wff2[1], S_wff2[1])}
                for g, k1 in ((2, 2), (3, 0), (4, 1)):
                    view = wff1[k1][:].rearrange("p k c -> p (k c)").rearrange("p (f d) -> p f d", d=1024)
                    t0_ring[g] = (view, B_wff1[k1], S_wff1[k1])
                    load_wff2_t0(g, t0_ring[g])
            def x1_affine(s):
                extra = ([b for ss in range(NS) for b in B_x1[ss]] + [B_h[31][1]]) if s == 0 else []
                pool.emit(lambda h: h.tensor_tensor(out=x1[:, s, :], in0=x1[:, s, :], in1=g1[:], op=ALU.mult),
                          reads=[B_consts] + extra, writes=B_x1[s])
                pool.emit(lambda h: h.tensor_tensor(out=x1[:, s, :], in0=x1[:, s, :], in1=b1[:], op=ALU.add),
                          reads=[B_consts], writes=B_x1[s])

            if i > 0:
                for s in range(NS):
                    x1_affine(s)
            def ff2_round(subtiles, last_round, n_groups=8):
                for g in range(n_groups):
                    slot = wff2_slots.pop(g)
                    if g == 7:
                        order = [(fl, s) for s in subtiles for fl in range(4)]
                    else:
                        order = [(fl, s) for fl in range(4) for s in subtiles]
                    for idx, (fl, s) in enumerate(order):
                        f = 4 * g + fl
                        for n in range(2):
                            bi = 2 * s + n
                            pe.emit(lambda h, bi=bi, f=f, fl=fl, s=s, n=n, slot=slot: h.matmul(
                                bank[bi][:], lhsT=hT[:, f, 128 * s:128 * s + 128],
                                rhs=wff2[slot][:, fl, 512 * n:512 * n + 512], start=(f == 0), stop=(f == 31)),
                                reads=[B_h[f][s // 2], B_wff2[slot]], writes=[B_bank[bi]],
                                signal=(f == 31 or (idx == len(order) - 1 and n == 1)))
                    if g + 2 < 8:
                        wff2_slots[g + 2] = load_wff2(g + 2)
                    elif not last_round:
                        wff2_slots[g + 2 - 8] = load_wff2(g + 2 - 8)
                    if i == 0 and g < NS:
                        x1_affine(g)

            def ln2_pieces(s, i):
                st_ = {}

                def p1():
                    st_["p"] = ln_stats(vbuf[s][:], B_vbuf[s])

                def p2():
                    p = st_["p"]
                    dve.emit(lambda h: h.scalar_tensor_tensor(out=vbuf[s][:], in0=vbuf[s][:], scalar=mv[p][:, 0:1],
                                                              in1=g2[:], op0=ALU.subtract, op1=ALU.mult),
                             reads=[B_mv[p], B_consts], writes=B_vbuf[s])

                def p3():
                    p = st_["p"]
                    dve.emit(lambda h: h.scalar_tensor_tensor(out=vbuf[s][:], in0=vbuf[s][:], scalar=rstd[p][:, 0:1],
                                                              in1=b2[:], op0=ALU.mult, op1=ALU.add),
                             reads=[B_rstd[p], B_consts], writes=B_vbuf[s])
                    r0 = T * i + 128 * s
                    tok = sp.dma(lambda h: h.dma_start(out=y[r0:r0 + 128, :], in_=vbuf[s][:]), S_y[s],
                                 reads=B_vbuf[s])
                    if i == NT - 1:
                        store_toks.append(tok)
                return [p1, p2, p3]

            if i == 0:
                wff2_slots.pop(0)
                wff2_slots.pop(1)
                for g in range(8):
                    wt, wb, ws = t0_ring[g]
                    for fl in range(4):
                        f = 4 * g + fl
                        for s in range(NS):
                            for n in range(2):
                                bi = 2 * s + n
                                pe.emit(lambda h, bi=bi, f=f, fl=fl, s=s, n=n, wt=wt: h.matmul(
                                    bank[bi][:], lhsT=hT[:, f, 128 * s:128 * s + 128],
                                    rhs=wt[:, fl, 512 * n:512 * n + 512], start=(f == 0), stop=(f == 31)),
                                    reads=[B_h[f][s // 2], wb], writes=[B_bank[bi]],
                                    signal=(f == 31 or (fl == 3 and s == NS - 1 and n == 1)))
                    if g + 5 < 8:
                        t0_ring[g + 5] = t0_ring[g]
                        load_wff2_t0(g + 5, t0_ring[g])
                    if g < NS:
                        x1_affine(g)
                alias_barrier(mix_bufs(), B_h_all)
                for s in range(NS):
                    ln_evac((2 * s, 2 * s + 1), x1[:, s, :], B_x1[s], vbuf[s][:], B_vbuf[s])
                for s in range(NS):
                    deferred_ln2.extend(ln2_pieces(s, i))
            elif i < NT - 1:
                ff2_round(list(range(NS)), True)
                alias_barrier(mix_bufs(), B_h_all)
                for s in range(NS):
                    ln_evac((2 * s, 2 * s + 1), x1[:, s, :], B_x1[s], vbuf[s][:], B_vbuf[s])
                for s in range(NS):
                    deferred_ln2.extend(ln2_pieces(s, i))
            else:
                tail_w = {}
                for g, k1 in ((6, 0), (7, 1)):
                    view = wff1[k1][:].rearrange("p k c -> p (k c)").rearrange("p (f d) -> p f d", d=1024)
                    sp.dma(lambda h, g=g, view=view: h.dma_start(out=view, in_=scr_ff2[g]), S_wff1[k1],
                           reads=[B_scr_ff2[g]], writes=[B_wff1[k1]])
                    tail_w[g] = (view, B_wff1[k1])
                ff2_round(list(range(NS)), True, n_groups=4)
                for g in (4, 5):
                    sl = wff2_slots.pop(g)
                    tail_w[g] = (wff2[sl][:], B_wff2[sl])
                for s in range(NS):
                    for f in range(16, 32):
                        wt, wb = tail_w[f // 4]
                        for n in range(2):
                            bi = 2 * s + n
                            pe.emit(lambda h, bi=bi, f=f, s=s, n=n, wt=wt: h.matmul(
                                bank[bi][:], lhsT=hT[:, f, 128 * s:128 * s + 128],
                                rhs=wt[:, f % 4, 512 * n:512 * n + 512], start=False, stop=(f == 31)),
                                reads=[B_h[f][s // 2], wb], writes=[B_bank[bi]], signal=(f == 31))
                    ln_evac((2 * s, 2 * s + 1), x1[:, s, :], B_x1[s], vbuf[s][:], B_vbuf[s])
                    for piece in ln2_pieces(s, i):
                        piece()

        for tok in store_toks:
            sp.wait_tok(tok)

        with nc.Block() as block:
            @block.tensor
            def _(h):
                pe.replay(h)

            @block.scalar
            def _(h):
                act.replay(h)

            @block.vector
            def _(h):
                dve.replay(h)

            @block.gpsimd
            def _(h):
                pool.replay(h)

            @block.sync
            def _(h):
                sp.replay(h)
    return nc


def _core_inputs(c, x_prompt, x_sample):
    if c < 4:
        seq = x_prompt[c]
        start = 0
    else:
        seq = x_sample[0]
        start = (c - 4) * NTOK
    S = seq.shape[0]
    xe = np.zeros((XE_ROWS, D), np.float32)
    lo = start - 8
    hi = start + NTOK + 8
    slo, shi = max(lo, 0), min(hi, S)
    xe[slo - lo:shi - lo] = seq[slo:shi]
    ecnt = np.zeros((4, 16), np.float32)
    for g, w in enumerate(POOL_W):
        for idx in range(16):
            t = start + (idx if idx < 8 else NTOK - 16 + idx)
            l = max(t - w // 2, 0)
            h = min(t + w - w // 2, S)
            ecnt[g, idx] = 1.0 / float(h - l)
    ecnt_r = np.ascontiguousarray(np.broadcast_to(ecnt.reshape(1, 64), (128, 64))).astype(np.float32)
    xet = np.empty((NT, 128, 8, TE), np.float32)
    for i in range(NT):
        xet[i] = xe[T * i:T * i + TE, :].reshape(TE, 8, 128).transpose(2, 1, 0)
    return xe, xet, ecnt_r


_NC_CACHE = {}


def kernel(x_prompt, x_sample, w_in, conv_w, pool_w, pool_scale, w_out, ln1_g, ln1_b,
           w_ff1, w_ff2, ln2_g, ln2_b):
    f = lambda a: np.ascontiguousarray(np.asarray(a, dtype=np.float32))
    x_prompt, x_sample = f(x_prompt), f(x_sample)
    w_in0, w_out0, w_ff10, w_ff20 = f(w_in)[0], f(w_out)[0], f(w_ff1)[0], f(w_ff2)[0]
    cw = np.ascontiguousarray(f(conv_w)[0].reshape(3, 4, 128).transpose(2, 1, 0).reshape(128, 12))
    pw = f(pool_w)[0]
    psc = np.ascontiguousarray(f(pool_scale)[0].reshape(4, 128).T)
    rep = lambda v: np.ascontiguousarray(np.broadcast_to(f(v)[0].reshape(1, D), (128, D)))
    g1r, b1r, g2r, b2r = rep(ln1_g), rep(ln1_b), rep(ln2_g), rep(ln2_b)
    col = lambda v: np.ascontiguousarray(f(v)[0].reshape(8, 128).T)
    g1c, b1c = col(ln1_g), col(ln1_b)

    if "nc" not in _NC_CACHE:
        _NC_CACHE["nc"] = build_program()
    nc = _NC_CACHE["nc"]

    in_maps = []
    for c in range(N_CORES):
        xe, xet, ecnt_r = _core_inputs(c, x_prompt, x_sample)
        in_maps.append({"xe": xe, "xet": xet, "w_in": w_in0, "w_out": w_out0, "w_ff1": w_ff10, "w_ff2": w_ff20,
                        "cw": cw, "pw": pw, "psc": psc, "ecnt": ecnt_r,
                        "g1r": g1r, "b1r": b1r, "g2r": g2r, "b2r": b2r, "g1c": g1c, "b1c": b1c})
    res = run_bass_kernel_spmd(nc, in_maps, core_ids=list(range(N_CORES)))
    outs = [np.asarray(r["y"], dtype=np.float32) for r in res.results]
    y_prompt = np.stack(outs[0:4], axis=0)
    y_sample = np.concatenate(outs[4:8], axis=0)[None]
    return (y_prompt, y_sample)
```

```python
from contextlib import ExitStack

import numpy as np
import concourse.bass as bass
import concourse.mybir as mybir
from concourse.bass_utils import run_bass_kernel_spmd

F32 = mybir.dt.float32
BF16 = mybir.dt.bfloat16
AF = mybir.ActivationFunctionType
ALU = mybir.AluOpType

D = 1024
DFF = 4096
NTOK = 4096
T = 512
NT = NTOK // T
NS = T // 128
TE = T + 16
XE_ROWS = NTOK + 16
ALPHA = 2.0 ** 0.25
EPS = 1e-5
POOL_W = (2, 4, 8, 16)
N_CORES = 8


class Buf:
    __slots__ = ("name", "w", "r", "const")

    def __init__(self, name, const=False):
        self.name = name
        self.w = None
        self.r = []
        self.const = const


class Eng:
    def __init__(self, name, sem, same_engine_sync=True):
        self.name = name
        self.sem = sem
        self.cnt = 0
        self.seen = {}
        self.ops = []
        self.same_engine_sync = same_engine_sync

    def _wait_for(self, toks):
        need = {}
        for tok in toks:
            if tok is None:
                continue
            sem, val = tok
            if sem is self.sem and not self.same_engine_sync:
                continue
            if self.seen.get(id(sem), 0) >= val:
                continue
            if need.get(id(sem), (None, 0))[1] < val:
                need[id(sem)] = (sem, val)
        for sem, val in need.values():
            self.ops.append(("wait", sem, val))
            self.seen[id(sem)] = val

    @staticmethod
    def _deps(reads, writes):
        toks = []
        for b in reads:
            toks.append(b.w)
        for b in writes:
            toks.append(b.w)
            toks.extend(b.r)
        return toks

    @staticmethod
    def _record(tok, reads, writes):
        for b in reads:
            if not b.const:
                b.r.append(tok)
        for b in writes:
            b.w = tok
            b.r = []

    def emit(self, fn, reads=(), writes=(), signal=True, excl=()):
        self._wait_for(self._deps(reads, writes))
        own = self.sem
        self._wait_for([t for t in self._deps((), excl) if t is not None and t[0] is not own])
        writes = list(writes) + list(excl)
        if signal:
            self.cnt += 1
            tok = (self.sem, self.cnt)
        else:
            tok = (self.sem, self.cnt + 1)
        self.ops.append(("ins", fn, self.sem if signal else None, 1))
        self._record(tok, reads, writes)

    def dma(self, fn, sem_state, reads=(), writes=()):
        self._wait_for(self._deps(reads, writes))
        sem_state[1] += 16
        tok = (sem_state[0], sem_state[1])
        self.ops.append(("ins", fn, sem_state[0], 16))
        self._record(tok, reads, writes)
        return tok

    def dma_multi(self, fns, sem_state, reads=(), writes=()):
        self._wait_for(self._deps(reads, writes))
        for fn in fns:
            sem_state[1] += 16
            self.ops.append(("ins", fn, sem_state[0], 16))
        tok = (sem_state[0], sem_state[1])
        self._record(tok, reads, writes)
        return tok

    def wait_tok(self, tok):
        self._wait_for([tok])

    def replay(self, h):
        for op in self.ops:
            if op[0] == "wait":
                h.wait_ge(op[1], op[2])
            else:
                ins = op[1](h)
                if op[2] is not None:
                    ins.then_inc(op[2], op[3])


def alias_barrier(new_bufs, old_bufs):
    toks = []
    for b in old_bufs:
        if b.w is not None:
            toks.append(b.w)
        toks.extend(b.r)
    for b in new_bufs:
        b.r = list(b.r) + toks


def build_program():
    nc = bass.Bass("TRN2", target_bir_lowering=False)

    def din(name, shape, dt=F32):
        return nc.dram_tensor(name, list(shape), dt, kind="ExternalInput").ap()

    xe = din("xe", [XE_ROWS, D])
    xet = din("xet", [NT, 128, 8, TE])
    w_in = din("w_in", [D, 2048])
    w_out = din("w_out", [D, D])
    w_ff1 = din("w_ff1", [D, DFF])
    w_ff2 = din("w_ff2", [DFF, D])
    cw_d = din("cw", [128, 12])
    pw_d = din("pw", [4, 128, 128])
    psc_d = din("psc", [128, 4])
    ecnt_d = din("ecnt", [128, 64])
    g1_d = din("g1r", [128, D])
    b1_d = din("b1r", [128, D])
    g1c_d = din("g1c", [128, 8])
    b1c_d = din("b1c", [128, 8])
    g2_d = din("g2r", [128, D])
    b2_d = din("b2r", [128, D])
    y = nc.dram_tensor("y", [NTOK, D], F32, kind="ExternalOutput").ap()

    scr_ff1 = nc.dram_tensor("scr_ff1", [8, 128, 8, 512], BF16).ap()
    scr_ff2 = nc.dram_tensor("scr_ff2", [8, 128, 4, 1024], BF16).ap()

    with ExitStack() as es:
        E = es.enter_context

        def sb(name, shape, dt=F32):
            return E(nc.sbuf_tensor(name, list(shape), dt))

        ident = sb("ident", [128, 128])
        cw = sb("cw_s", [128, 12])
        psc = sb("psc_s", [128, 4])
        ecnt = sb("ecnt_s", [128, 64])
        pw = sb("pw_s", [128, 4, 128], BF16)
        g1 = sb("g1_s", [128, D])
        b1 = sb("b1_s", [128, D])
        g1c = sb("g1c_s", [128, 8])
        b1c = sb("b1c_s", [128, 8])
        g2 = sb("g2_s", [128, D])
        b2 = sb("b2_s", [128, D])
        win_all = sb("win_all", [128, 8, 2048], BF16)
        wout = sb("wout", [128, 8, 1024], BF16)
        wff1 = [sb(f"wff1_{i}", [128, 8, 512], BF16) for i in range(3)]
        wff2 = [sb(f"wff2_{i}", [128, 4, 1024], BF16) for i in range(2)]
        xT = sb("xT", [128, 8, TE], BF16)
        xr = [sb(f"xr{i}", [128, D]) for i in range(2)]
        x1 = sb("x1", [128, NS, D])
        x1T = sb("x1T", [128, 8, T], BF16)
        rtmp = [sb(f"rtmp{i}", [128, T]) for i in range(2)]
        vbuf = [sb(f"vbuf{i}", [128, D]) for i in range(4)]
        st = [sb(f"st{i}", [128, 12]) for i in range(2)]
        mv = [sb(f"mv{i}", [128, 2]) for i in range(2)]
        sd = [sb(f"sd{i}", [128, 1]) for i in range(2)]
        rstd = [sb(f"rstd{i}", [128, 1]) for i in range(2)]
        tmp8 = sb("tmp8", [128, 8])
        nhalf = sb("nhalf", [128, 1])
        R = sb("R", [128, 8448])
        hT = R[:, 0:8192].bitcast(BF16).rearrange("p (f t) -> p f t", t=T)
        vS = R[:, 2112:2640]
        gcv = R[:, 2640:3168]
        acc = R[:, 3168:3696]
        pS = R[:, 3696:4224]
        Aa = R[:, 4224:4752]
        Ab = R[:, 4752:5280]
        pooled = R[:, 5280:6304].bitcast(BF16).rearrange("p (g t) -> p g t", t=T)
        ycat = R[:, 6304:8352].bitcast(BF16).rearrange("p (k t) -> p k t", t=T)
        bank = [E(nc.psum_tensor(f"bank{i}", [128, 512], F32)) for i in range(8)]

        def sem(name):
            return E(nc.semaphore(name))

        pe = Eng("pe", sem("s_pe"), same_engine_sync=False)
        act = Eng("act", sem("s_act"))
        dve = Eng("dve", sem("s_dve"))
        pool = Eng("pool", sem("s_pool"))
        sp = Eng("sp", sem("s_sp"), same_engine_sync=False)

        B_ident = Buf("ident")
        B_consts = Buf("consts")
        B_pw = Buf("pw")
        B_win = [Buf("win_lo"), Buf("win_hi")]
        B_win_lo = [Buf("win_lo_a"), Buf("win_lo_b")]
        B_stage = Buf("w_in_stage")
        B_wout = Buf("wout")
        B_wff1 = [Buf(f"wff1_{i}") for i in range(3)]
        B_wff2 = [Buf(f"wff2_{i}") for i in range(2)]
        B_xr = [Buf(f"xr{i}") for i in range(2)]
        B_x1 = [[Buf(f"x1_{s}_{n}") for n in range(2)] for s in range(NS)]
        B_x1T = [[Buf(f"x1T_{s}_{k}") for k in range(8)] for s in range(NS)]
        B_rtmp = [Buf(f"rtmp{i}") for i in range(4)]
        B_vbuf = [[Buf(f"vbuf{i}_{n}") for n in range(2)] for i in range(4)]
        B_st = [[Buf(f"st{i}_{n}") for n in range(2)] for i in range(2)]
        B_mv = [Buf(f"mv{i}") for i in range(2)]
        B_sd = [Buf(f"sd{i}") for i in range(2)]
        B_rstd = [Buf(f"rstd{i}") for i in range(2)]
        B_tmp8 = Buf("tmp8")
        B_h = [[Buf(f"h{f}_{hf}") for hf in range(2)] for f in range(32)]
        B_h_all = [b for pair in B_h for b in pair]
        B_xT = Buf("xT")
        B_vS = [Buf("vS0"), Buf("vS1")]
        B_gcv = [Buf("gcv0"), Buf("gcv1")]
        B_pS = [Buf("pS0"), Buf("pS1")]
        B_acc, B_Aa, B_Ab = Buf("acc"), Buf("Aa"), Buf("Ab")
        B_pooled = [Buf(f"pooled{g}") for g in range(4)]
        B_ycat = [Buf(f"ycat{k}") for k in range(8)]
        B_bank = [Buf(f"bank{i}") for i in range(8)]
        B_scr_ff1 = [Buf(f"scr_ff1_{g}") for g in range(8)]
        B_scr_ff2 = [Buf(f"scr_ff2_{g}") for g in range(8)]

        def mix_bufs():
            out = B_vS + B_gcv + B_pS + [B_acc, B_Aa, B_Ab] + B_pooled + B_ycat
            return out

        def dsem(name):
            return [sem(name), 0]

        S_const = dsem("d_const")
        S_pw = dsem("d_pw")
        S_scr_ff1 = [dsem(f"d_sff1_{g}") for g in range(8)]
        S_scr_ff2 = [dsem(f"d_sff2_{g}") for g in range(8)]
        S_win = [dsem("d_win_lo"), dsem("d_win_hi")]
        S_wout = dsem("d_wout")
        S_wff1 = [dsem(f"d_wff1_{i}") for i in range(3)]
        S_wff2 = [dsem(f"d_wff2_{i}") for i in range(2)]
        S_xT = dsem("d_xT")
        S_xr = [dsem(f"d_xr{i}") for i in range(2)]
        S_y = [dsem(f"d_y{i}") for i in range(4)]

        pool.emit(lambda h: h.memset(ident[:], 0.0), writes=[B_ident])
        pool.emit(lambda h: h.affine_select(out=ident[:], in_=ident[:], pattern=[[-1, 128]],
                                            compare_op=ALU.not_equal, fill=1.0, base=0,
                                            channel_multiplier=1), writes=[B_ident])
        pool.emit(lambda h: h.memset(nhalf[:], -0.5), writes=[B_ident])
        pool.dma(lambda h: h.dma_start(out=pw[:], in_=pw_d.rearrange("g c d -> c g d")), S_pw, writes=[B_pw])
        def cast_chain():
            hist = []

            def one(fn, sem_state, buf):
                if len(hist) >= 2:
                    pool.wait_tok(hist[-2])
                hist.append(pool.dma(fn, sem_state, writes=[buf]))

            one(lambda h: h.dma_start(out=wout[:], in_=w_out.rearrange("(k p) d -> p k d", p=128)), S_wout, B_wout)
            B_wout.const = True

        B_win_hi = [Buf("win_hi_a")] + [Buf(f"win_hi_b{kk}") for kk in range(4)]
        x1_all = [b for ss in range(NS) for b in B_x1[ss]]
        sp.dma(lambda h: h.dma_start(out=x1[:, 0:4, :],
                                     in_=w_in[0:512, 1024:2048].rearrange("(k p) c -> p k c", p=128)),
               S_win[1], writes=x1_all)
        for kk in range(4):
            sp.dma(lambda h, kk=kk: h.dma_start(out=vbuf[kk][:], in_=w_in[(4 + kk) * 128:(5 + kk) * 128, 1024:2048]),
                   S_y[kk], writes=B_vbuf[kk])
        act.emit(lambda h: h.activation(out=win_all[:, 0:4, 1024:2048], in_=x1[:, 0:4, :], func=AF.Copy),
                 reads=x1_all, writes=[B_win_hi[0]])
        for kk in range(4):
            act.emit(lambda h, kk=kk: h.activation(out=win_all[:, 4 + kk, 1024:2048], in_=vbuf[kk][:], func=AF.Copy),
                     reads=B_vbuf[kk], writes=[B_win_hi[1 + kk]])
        for b_ in B_win_hi:
            b_.const = True

        stage = R[:, 0:8192].rearrange("p (k c) -> p k c", c=1024)
        sp.dma(lambda h: h.dma_start(out=stage, in_=w_in[:, 0:1024].rearrange("(k p) c -> p k c", p=128)),
               S_win[0], writes=[B_stage])
        dve.emit(lambda h: h.tensor_copy(win_all[:, 0:4, 0:1024], stage[:, 0:4, :]),
                 reads=[B_stage], writes=[B_win_lo[0]])
        dve.emit(lambda h: h.tensor_copy(win_all[:, 4:8, 0:1024], stage[:, 4:8, :]),
                 reads=[B_stage], writes=[B_win_lo[1]])
        B_win_lo[0].const = True
        B_win_lo[1].const = True
        alias_barrier(mix_bufs() + B_h_all, [B_stage])

        consts = ((cw, cw_d), (psc, psc_d), (ecnt, ecnt_d), (g1c, g1c_d), (b1c, b1c_d), (g1, g1_d), (b1, b1_d),
                  (g2, g2_d), (b2, b2_d))
        sp.dma_multi([(lambda h, dst=dst, src=src: h.dma_start(out=dst[:], in_=src)) for dst, src in consts],
                     S_const, writes=[B_consts])
        B_consts.const = True
        B_ident.const = True
        B_pw.const = True

        xr_ctr = [0]
        win_ctr = [0]
        wff1_ctr = [0]
        wff2_ctr = [0]
        ln_ctr = [0]

        def load_xT(i):
            pool.dma(lambda h: h.dma_start(out=xT[:], in_=xet[i]), S_xT, writes=[B_xT])

        def load_xr(i, s):
            slot = xr_ctr[0] % 2
            xr_ctr[0] += 1
            r0 = 8 + T * i + 128 * s
            sp.dma(lambda h: h.dma_start(out=xr[slot][:], in_=xe[r0:r0 + 128, :]), S_xr[slot], writes=[B_xr[slot]])
            return slot

        first_pass = [True]

        def load_wff1(g):
            slot = wff1_ctr[0] % 3
            wff1_ctr[0] += 1
            if first_pass[0]:
                pool.dma(lambda h: h.dma_start(
                    out=wff1[slot][:], in_=w_ff1[:, g * 512:(g + 1) * 512].rearrange("(k p) c -> p k c", p=128)),
                    S_wff1[slot], writes=[B_wff1[slot]])
                sp.dma(lambda h: h.dma_start(out=scr_ff1[g], in_=wff1[slot][:]), S_scr_ff1[g],
                       reads=[B_wff1[slot]], writes=[B_scr_ff1[g]])
            else:
                assert B_scr_ff1[g].w is not None, "w_ff1 scratch group read before it was written"
                sp.dma(lambda h: h.dma_start(out=wff1[slot][:], in_=scr_ff1[g]), S_wff1[slot],
                       reads=[B_scr_ff1[g]], writes=[B_wff1[slot]])
            return slot

        def load_wff2(g):
            slot = wff2_ctr[0] % 2
            wff2_ctr[0] += 1
            if first_pass[0]:
                pool.dma(lambda h: h.dma_start(
                    out=wff2[slot][:], in_=w_ff2[g * 512:(g + 1) * 512, :].rearrange("(f p) d -> p f d", p=128)),
                    S_wff2[slot], writes=[B_wff2[slot]])
                sp.dma(lambda h: h.dma_start(out=scr_ff2[g], in_=wff2[slot][:]), S_scr_ff2[g],
                       reads=[B_wff2[slot]], writes=[B_scr_ff2[g]])
            else:
                assert B_scr_ff2[g].w is not None, "w_ff2 scratch group read before it was written"
                sp.dma(lambda h: h.dma_start(out=wff2[slot][:], in_=scr_ff2[g]), S_wff2[slot],
                       reads=[B_scr_ff2[g]], writes=[B_wff2[slot]])
            return slot

        def load_wff2_t0(g, slot3):
            wt, wb, ws = slot3
            pool.dma(lambda h: h.dma_start(
                out=wt, in_=w_ff2[g * 512:(g + 1) * 512, :].rearrange("(f p) d -> p f d", p=128)), ws, writes=[wb])
            sp.dma(lambda h: h.dma_start(out=scr_ff2[g], in_=wt), S_scr_ff2[g], reads=[wb], writes=[B_scr_ff2[g]])

        def ln_evac(src_banks, resid_ap, resid_bufs, v, vbufs):
            for n in range(2):
                bi = src_banks[n]
                dve.emit(lambda h, n=n, bi=bi: h.scalar_tensor_tensor(
                    out=v[:, n * 512:(n + 1) * 512], in0=resid_ap[:, n * 512:(n + 1) * 512], scalar=ALPHA,
                    in1=bank[bi][:], op0=ALU.mult, op1=ALU.add),
                    reads=resid_bufs, writes=[vbufs[n]], excl=[B_bank[bi]])

        def ln_stats(v, vbufs, use_pool=True):
            p = ln_ctr[0] % 2
            ln_ctr[0] += 1
            for n in range(2):
                dve.emit(lambda h, n=n: h.bn_stats(out=st[p][:, n * 6:(n + 1) * 6], in_=v[:, n * 512:(n + 1) * 512]),
                         reads=[vbufs[n]], writes=[B_st[p][n]])
            dve.emit(lambda h: h.bn_aggr(out=mv[p][:], in_=st[p][:]), reads=B_st[p], writes=[B_mv[p]])
            if not use_pool:
                act.emit(lambda h: h.activation(out=sd[p][:], in_=mv[p][:, 1:2], func=AF.Sqrt, bias=EPS, scale=1.0),
                         reads=[B_mv[p]], writes=[B_sd[p]])
                dve.emit(lambda h: h.reciprocal(out=rstd[p][:], in_=sd[p][:]), reads=[B_sd[p]], writes=[B_rstd[p]])
                return p
            pool.emit(lambda h: h.tensor_scalar(out=sd[p][:], in0=mv[p][:, 1:2], scalar1=EPS, scalar2=None,
                                                op0=ALU.add),
                      reads=[B_mv[p]], writes=[B_sd[p]])
            pool.emit(lambda h: h.tensor_tensor(out=rstd[p][:], in0=sd[p][:], in1=nhalf[:], op=ALU.pow),
                      reads=[B_sd[p], B_ident], writes=[B_rstd[p]])
            return p

        def ln_norm(v, vbufs, p):
            dve.emit(lambda h: h.tensor_scalar(out=v, in0=v, scalar1=mv[p][:, 0:1], scalar2=rstd[p][:, 0:1],
                                               op0=ALU.subtract, op1=ALU.mult),
                     reads=[B_mv[p], B_rstd[p]], writes=vbufs)

        def ln_affine(v, vbufs, p, gam, bet):
            dve.emit(lambda h: h.scalar_tensor_tensor(out=v, in0=v, scalar=mv[p][:, 0:1], in1=gam[:],
                                                      op0=ALU.subtract, op1=ALU.mult),
                     reads=[B_mv[p], B_consts], writes=vbufs)
            dve.emit(lambda h: h.scalar_tensor_tensor(out=v, in0=v, scalar=rstd[p][:, 0:1], in1=bet[:],
                                                      op0=ALU.mult, op1=ALU.add),
                     reads=[B_rstd[p], B_consts], writes=vbufs)

        load_xT(0)
        cast_chain()
        wff1_slots = {}
        wff2_slots = {}
        store_toks = []
        deferred_ln2 = []

        for i in range(NT):
            first_pass[0] = (i == 0)
            xT_all = [B_xT]

            xr_slots = {0: load_xr(i, 0), 1: load_xr(i, 1)}
            for g in range(3):
                wff1_slots[("A", g)] = load_wff1(g)

            main_rr = [0]
            ext_rr = [0]

            HALF = TE // 2

            def proj_chunk(cb, jj, with_ext, main_lo):
                lhs = lambda k: win_all[:, k, cb * 512 + jj * 128:cb * 512 + (jj + 1) * 128]
                if not with_ext:
                    mb = main_rr[0] % 7
                    main_rr[0] += 1
                    for k in range(8):
                        pe.emit(lambda h, k=k: h.matmul(
                            bank[mb][:], lhsT=lhs(k), rhs=xT[:, k, main_lo:main_lo + 512],
                            start=(k == 0), stop=(k == 7)),
                            reads=(B_win_lo if cb < 2 else B_win_hi) + xT_all, writes=[B_bank[mb]], signal=(k == 7))
                    return mb, None
                b0 = main_rr[0] % 7
                b1 = (main_rr[0] + 1) % 7
                main_rr[0] += 2
                for k in range(8):
                    for bb, c0 in ((b0, 0), (b1, HALF)):
                        pe.emit(lambda h, k=k, bb=bb, c0=c0: h.matmul(
                            bank[bb][:, 0:HALF], lhsT=lhs(k), rhs=xT[:, k, c0:c0 + HALF],
                            start=(k == 0), stop=(k == 7)),
                            reads=(B_win_lo if cb < 2 else B_win_hi) + xT_all, writes=[B_bank[bb]], signal=(k == 7 and bb == b1))
                return b0, b1

            pending_poolmix = []

            def do_poolmix(g):
                pe.emit(lambda h: h.matmul(bank[7][:], lhsT=pw[:, g, :], rhs=pooled[:, g, :], start=True, stop=True),
                        reads=[B_pw, B_pooled[g]], writes=[B_bank[7]])
                act.emit(lambda h: h.activation(out=ycat[:, 4 + g, :], in_=bank[7][:], func=AF.Identity,
                                                scale=psc[:, g:g + 1]),
                         reads=[B_consts], writes=[B_ycat[4 + g]], excl=[B_bank[7]])

            sum_eng = dve if i == 0 else pool
            for j in range(4):
                pj = 3 - j
                mb_v, eb_v = proj_chunk(2, j, True, 0)
                act.emit(lambda h, mb=mb_v: h.activation(out=vS[:, 0:HALF], in_=bank[mb][:, 0:HALF], func=AF.Copy),
                         writes=[B_vS[0]], excl=[B_bank[mb_v]])
                act.emit(lambda h, eb=eb_v: h.activation(out=vS[:, HALF:TE], in_=bank[eb][:, 0:HALF], func=AF.Copy),
                         writes=[B_vS[1]], excl=[B_bank[eb_v]])
                mb_c, eb_c = proj_chunk(1, j, True, 0)
                dve.emit(lambda h, mb=mb_c: h.tensor_tensor(out=gcv[:, 0:HALF], in0=bank[mb][:, 0:HALF],
                                                            in1=vS[:, 0:HALF], op=ALU.mult),
                         reads=[B_vS[0]], writes=[B_gcv[0]], excl=[B_bank[mb_c]])
                dve.emit(lambda h, eb=eb_c: h.tensor_tensor(out=gcv[:, HALF:TE], in0=bank[eb][:, 0:HALF],
                                                            in1=vS[:, HALF:TE], op=ALU.mult),
                         reads=[B_vS[1]], writes=[B_gcv[1]], excl=[B_bank[eb_c]])
                act.emit(lambda h, j=j: h.activation(out=acc[:, 0:512], in_=gcv[:, 8:520], func=AF.Identity,
                                                     scale=cw[:, 3 * j + 1:3 * j + 2]),
                         reads=B_gcv + [B_consts], writes=[B_acc])
                dve.emit(lambda h, j=j: h.scalar_tensor_tensor(out=acc[:, 0:512], in0=gcv[:, 7:519],
                                                               scalar=cw[:, 3 * j:3 * j + 1], in1=acc[:, 0:512],
                                                               op0=ALU.mult, op1=ALU.add),
                         reads=B_gcv + [B_consts], writes=[B_acc])
                dve.emit(lambda h, j=j: h.scalar_tensor_tensor(out=acc[:, 0:512], in0=gcv[:, 9:521],
                                                               scalar=cw[:, 3 * j + 2:3 * j + 3], in1=acc[:, 0:512],
                                                               op0=ALU.mult, op1=ALU.add),
                         reads=B_gcv + [B_consts], writes=[B_acc])
                mb_b, _ = proj_chunk(0, j, False, 8)
                dve.emit(lambda h, mb=mb_b, j=j: h.tensor_tensor(out=ycat[:, j, :], in0=bank[mb][:], in1=acc[:, 0:512],
                                                                op=ALU.mult),
                         reads=[B_acc], writes=[B_ycat[j]], excl=[B_bank[mb_b]])
                mb_p, eb_p = proj_chunk(3, pj, True, 0)
                act.emit(lambda h, mb=mb_p: h.activation(out=pS[:, 0:HALF], in_=bank[mb][:, 0:HALF], func=AF.Copy),
                         writes=[B_pS[0]], excl=[B_bank[mb_p]])
                act.emit(lambda h, eb=eb_p: h.activation(out=pS[:, HALF:TE], in_=bank[eb][:, 0:HALF], func=AF.Copy),
                         writes=[B_pS[1]], excl=[B_bank[eb_p]])
                w = POOL_W[pj]
                cur, cur_b, L = pS, B_pS, TE
                pp = [(Aa, [B_Aa]), (Ab, [B_Ab])]
                m = 1
                step = 0
                while m < w:
                    dst, dst_b = pp[step % 2]
                    L2 = L - m
                    sum_eng.emit(lambda h, dst=dst, cur=cur, L2=L2, m=m: h.tensor_tensor(
                        out=dst[:, 0:L2], in0=cur[:, 0:L2], in1=cur[:, m:m + L2], op=ALU.add),
                        reads=cur_b, writes=dst_b)
                    cur, cur_b, L = dst, dst_b, L2
                    m *= 2
                    step += 1
                off = 8 - w // 2
                dve.emit(lambda h, cur=cur, off=off, pj=pj, w=w: h.scalar_tensor_tensor(
                    out=pooled[:, pj, :], in0=cur[:, off:off + 512], scalar=1.0 / w, in1=pS[:, 8:520],
                    op0=ALU.mult, op1=ALU.subtract),
                    reads=cur_b + B_pS, writes=[B_pooled[pj]])
                if i == 0 or i == NT - 1:
                    q0 = 0 if i == 0 else 504
                    e0 = 16 * pj + (0 if i == 0 else 8)
                    dve.emit(lambda h, cur=cur, off=off, q0=q0, e0=e0: h.tensor_tensor(
                        out=tmp8[:], in0=cur[:, off + q0:off + q0 + 8], in1=ecnt[:, e0:e0 + 8], op=ALU.mult),
                        reads=cur_b + [B_consts], writes=[B_tmp8])
                    dve.emit(lambda h, q0=q0, pj=pj: h.tensor_tensor(
                        out=pooled[:, pj, q0:q0 + 8], in0=tmp8[:], in1=pS[:, 8 + q0:16 + q0], op=ALU.subtract),
                        reads=[B_tmp8] + B_pS, writes=[B_pooled[pj]])
                if pending_poolmix:
                    do_poolmix(pending_poolmix.pop(0))
                pending_poolmix.append(pj)
            while len(pending_poolmix) > 1:
                do_poolmix(pending_poolmix.pop(0))
            if i + 1 < NT:
                load_xT(i + 1)

            if i > 0:
                for g in range(2):
                    wff2_slots[g] = load_wff2(g)

            korder = [0, 7, 1, 6, 2, 5, 3, 4]

            def wo_mm(s, n, idx):
                k = korder[idx]
                bi = 2 * s + n
                pe.emit(lambda h: h.matmul(
                    bank[bi][:], lhsT=ycat[:, k, 128 * s:128 * s + 128],
                    rhs=wout[:, k, 512 * n:512 * n + 512], start=(idx == 0), stop=(idx == 7)),
                    reads=[B_ycat[k], B_wout], writes=[B_bank[bi]], signal=(idx == 7))

            for idx in range(7):
                for n in range(2):
                    wo_mm(0, n, idx)
            do_poolmix(pending_poolmix.pop(0))
            for n in range(2):
                wo_mm(0, n, 7)
            for idx in range(8):
                for n in range(2):
                    wo_mm(1, n, idx)

            def wo_pair1(idx_range):
                for idx in idx_range:
                    for s in (2, 3):
                        for n in range(2):
                            wo_mm(s, n, idx)

            def ln1(s):
                slot = xr_slots.pop(s)
                v = x1[:, s, :]
                ln_evac((2 * s, 2 * s + 1), xr[slot], [B_xr[slot]], v, B_x1[s])
                if s + 2 < NS:
                    xr_slots[s + 2] = load_xr(i, s + 2)
                p = ln_stats(v, B_x1[s], use_pool=(i > 0))
                ln_norm(v, B_x1[s], p)

            def tr1(s):
                for hh in range(2):
                    bi = (2 * s + hh) % 8
                    for kk in range(4):
                        k = 4 * hh + kk
                        pe.emit(lambda h, bi=bi, kk=kk, k=k: h.transpose(
                            bank[bi][:, kk * 128:(kk + 1) * 128], x1[:, s, k * 128:(k + 1) * 128], ident[:]),
                            reads=B_x1[s] + [B_ident], writes=[B_bank[bi]], signal=(kk == 3))
                    for kk in range(4):
                        k = 4 * hh + kk
                        if hh == 0:
                            act.emit(lambda h, bi=bi, kk=kk, k=k: h.activation(
                                out=x1T[:, k, 128 * s:128 * s + 128], in_=bank[bi][:, kk * 128:(kk + 1) * 128],
                                func=AF.Identity, scale=g1c[:, k:k + 1], bias=b1c[:, k:k + 1]),
                                reads=[B_consts], writes=[B_x1T[s][k]], excl=[B_bank[bi]])
                        else:
                            dve.emit(lambda h, bi=bi, kk=kk, k=k: h.tensor_scalar(
                                out=x1T[:, k, 128 * s:128 * s + 128], in0=bank[bi][:, kk * 128:(kk + 1) * 128],
                                scalar1=g1c[:, k:k + 1], scalar2=b1c[:, k:k + 1], op0=ALU.mult, op1=ALU.add),
                                reads=[B_consts], writes=[B_x1T[s][k]], excl=[B_bank[bi]])

            def ff1_group(hf, g, slot):
                x1T_half = [B_x1T[s][k] for s in (2 * hf, 2 * hf + 1) for k in range(8)]
                c0 = 256 * hf
                for fl in range(4):
                    f = 4 * g + fl
                    bi = f % (4 if hf == 0 else 8)
                    rp = f % 4
                    rt = rtmp[rp // 2][:, 256 * (rp % 2):256 * (rp % 2) + 256]
                    for k in range(8):
                        pe.emit(lambda h, bi=bi, k=k, fl=fl: h.matmul(
                            bank[bi][:, 0:256], lhsT=wff1[slot][:, k, fl * 128:(fl + 1) * 128],
                            rhs=x1T[:, k, c0:c0 + 256], start=(k == 0), stop=(k == 7)),
                            reads=[B_wff1[slot]] + x1T_half, writes=[B_bank[bi]], signal=(k == 7))
                    act.emit(lambda h, bi=bi, rt=rt: h.activation(out=rt, in_=bank[bi][:, 0:256], func=AF.Relu),
                             writes=[B_rtmp[rp]], excl=[B_bank[bi]])
                    dve.emit(lambda h, rt=rt, f=f: h.tensor_tensor(out=hT[:, f, c0:c0 + 256], in0=rt, in1=rt,
                                                                   op=ALU.mult),
                             reads=[B_rtmp[rp]], writes=[B_h[f][hf]])
                    if hf == 0 and f >= 12 and deferred_ln2:
                        deferred_ln2.pop(0)()

            ln1(0)
            ln1(1)
            wo_pair1(range(0, 2))
            tr1(0)
            wo_pair1(range(2, 6))
            tr1(1)
            wo_pair1(range(6, 8))
            alias_barrier(B_h_all, mix_bufs())
            ln1(2)
            ln1(3)
            if i > 0:
                first = [wff1_slots.pop(("A", g)) for g in range(3)]
                ff1_group(0, 0, first[0])
                ff1_group(0, 1, first[1])
                tr1(2)
                tr1(3)
                ff1_group(0, 2, first[2])
                for g in range(3):
                    ff1_group(1, g, first[g])
                    wff1_slots[("A", g + 3)] = load_wff1(g + 3)
                for g in range(3, 8):
                    slot = wff1_slots.pop(("A", g))
                    ff1_group(0, g, slot)
                    ff1_group(1, g, slot)
                    if g + 3 < 8:
                        wff1_slots[("A", g + 3)] = load_wff1(g + 3)
                while deferred_ln2:
                    deferred_ln2.pop(0)()
            else:
                for g in range(2):
                    wff2_slots[g] = load_wff2(g)
                tr1(2)
                tr1(3)
                for g in range(8):
                    slot = wff1_slots.pop(("A", g))
                    ff1_group(0, g, slot)
                    ff1_group(1, g, slot)
                    if g + 3 < 8:
                        wff1_slots[("A", g + 3)] = load_wff1(g + 3)
                t0_ring = {0: (wff2[0][:], B_wff2[0], S_wff2[0]), 1: (wff2[1][:], B_wff2[1], S_wff2[1])}
                for g, k1 in ((2, 2), (3, 0), (4, 1)):
                    view = wff1[k1][:].rearrange("p k c -> p (k c)").rearrange("p (f d) -> p f d", d=1024)
                    t0_ring[g] = (view, B_wff1[k1], S_wff1[k1])
                    load_wff2_t0(g, t0_ring[g])
            def x1_affine(s):
                extra = ([b for ss in range(NS) for b in B_x1[ss]] + [B_h[31][1]]) if s == 0 else []
                pool.emit(lambda h: h.tensor_tensor(out=x1[:, s, :], in0=x1[:, s, :], in1=g1[:], op=ALU.mult),
                          reads=[B_consts] + extra, writes=B_x1[s])
                pool.emit(lambda h: h.tensor_tensor(out=x1[:, s, :], in0=x1[:, s, :], in1=b1[:], op=ALU.add),
                          reads=[B_consts], writes=B_x1[s])

            if i > 0:
                for s in range(NS):
                    x1_affine(s)
            def ff2_round(subtiles, last_round, n_groups=8):
                for g in range(n_groups):
                    slot = wff2_slots.pop(g)
                    if g == 7:
                        order = [(fl, s) for s in subtiles for fl in range(4)]
                    else:
                        order = [(fl, s) for fl in range(4) for s in subtiles]
                    for idx, (fl, s) in enumerate(order):
                        f = 4 * g + fl
                        for n in range(2):
                            bi = 2 * s + n
                            pe.emit(lambda h, bi=bi, f=f, fl=fl, s=s, n=n, slot=slot: h.matmul(
                                bank[bi][:], lhsT=hT[:, f, 128 * s:128 * s + 128],
                                rhs=wff2[slot][:, fl, 512 * n:512 * n + 512], start=(f == 0), stop=(f == 31)),
                                reads=[B_h[f][s // 2], B_wff2[slot]], writes=[B_bank[bi]],
                                signal=(f == 31 or (idx == len(order) - 1 and n == 1)))
                    if g + 2 < 8:
                        wff2_slots[g + 2] = load_wff2(g + 2)
                    elif not last_round:
                        wff2_slots[g + 2 - 8] = load_wff2(g + 2 - 8)
                    if i == 0 and g < NS:
                        x1_affine(g)

            def ln2_pieces(s, i):
                st_ = {}

                def p1():
                    st_["p"] = ln_stats(vbuf[s][:], B_vbuf[s])

                def p2():
                    p = st_["p"]
                    dve.emit(lambda h: h.scalar_tensor_tensor(out=vbuf[s][:], in0=vbuf[s][:], scalar=mv[p][:, 0:1],
                                                              in1=g2[:], op0=ALU.subtract, op1=ALU.mult),
                             reads=[B_mv[p], B_consts], writes=B_vbuf[s])

                def p3():
                    p = st_["p"]
                    dve.emit(lambda h: h.scalar_tensor_tensor(out=vbuf[s][:], in0=vbuf[s][:], scalar=rstd[p][:, 0:1],
                                                              in1=b2[:], op0=ALU.mult, op1=ALU.add),
                             reads=[B_rstd[p], B_consts], writes=B_vbuf[s])
                    r0 = T * i + 128 * s
                    tok = sp.dma(lambda h: h.dma_start(out=y[r0:r0 + 128, :], in_=vbuf[s][:]), S_y[s],
                                 reads=B_vbuf[s])
                    if i == NT - 1:
                        store_toks.append(tok)
                return [p1, p2, p3]

            if i == 0:
                wff2_slots.pop(0)
                wff2_slots.pop(1)
                for g in range(8):
                    wt, wb, ws = t0_ring[g]
                    for fl in range(4):
                        f = 4 * g + fl
                        for s in range(NS):
                            for n in range(2):
                                bi = 2 * s + n
                                pe.emit(lambda h, bi=bi, f=f, fl=fl, s=s, n=n, wt=wt: h.matmul(
                                    bank[bi][:], lhsT=hT[:, f, 128 * s:128 * s + 128],
                                    rhs=wt[:, fl, 512 * n:512 * n + 512], start=(f == 0), stop=(f == 31)),
                                    reads=[B_h[f][s // 2], wb], writes=[B_bank[bi]],
                                    signal=(f == 31 or (fl == 3 and s == NS - 1 and n == 1)))
                    if g + 5 < 8:
                        t0_ring[g + 5] = t0_ring[g]
                        load_wff2_t0(g + 5, t0_ring[g])
                    if g < NS:
                        x1_affine(g)
                alias_barrier(mix_bufs(), B_h_all)
                for s in range(NS):
                    ln_evac((2 * s, 2 * s + 1), x1[:, s, :], B_x1[s], vbuf[s][:], B_vbuf[s])
                for s in range(NS):
                    deferred_ln2.extend(ln2_pieces(s, i))
            elif i < NT - 1:
                ff2_round(list(range(NS)), True)
                alias_barrier(mix_bufs(), B_h_all)
                for s in range(NS):
                    ln_evac((2 * s, 2 * s + 1), x1[:, s, :], B_x1[s], vbuf[s][:], B_vbuf[s])
                for s in range(NS):
                    deferred_ln2.extend(ln2_pieces(s, i))
            else:
                tail_w = {}
                for g, k1 in ((6, 0), (7, 1)):
                    view = wff1[k1][:].rearrange("p k c -> p (k c)").rearrange("p (f d) -> p f d", d=1024)
                    sp.dma(lambda h, g=g, view=view: h.dma_start(out=view, in_=scr_ff2[g]), S_wff1[k1],
                           reads=[B_scr_ff2[g]], writes=[B_wff1[k1]])
                    tail_w[g] = (view, B_wff1[k1])
                ff2_round(list(range(NS)), True, n_groups=4)
                for g in (4, 5):
                    sl = wff2_slots.pop(g)
                    tail_w[g] = (wff2[sl][:], B_wff2[sl])
                for s in range(NS):
                    for f in range(16, 32):
                        wt, wb = tail_w[f // 4]
                        for n in range(2):
                            bi = 2 * s + n
                            pe.emit(lambda h, bi=bi, f=f, s=s, n=n, wt=wt: h.matmul(
                                bank[bi][:], lhsT=hT[:, f, 128 * s:128 * s + 128],
                                rhs=wt[:, f % 4, 512 * n:512 * n + 512], start=False, stop=(f == 31)),
                                reads=[B_h[f][s // 2], wb], writes=[B_bank[bi]], signal=(f == 31))
                    ln_evac((2 * s, 2 * s + 1), x1[:, s, :], B_x1[s], vbuf[s][:], B_vbuf[s])
                    for piece in ln2_pieces(s, i):
                        piece()

        for tok in store_toks:
            sp.wait_tok(tok)

        with nc.Block() as block:
            @block.tensor
            def _(h):
                pe.replay(h)

            @block.scalar
            def _(h):
                act.replay(h)

            @block.vector
            def _(h):
                dve.replay(h)

            @block.gpsimd
            def _(h):
                pool.replay(h)

            @block.sync
            def _(h):
                sp.replay(h)
    return nc


def _core_inputs(c, x_prompt, x_sample):
    if c < 4:
        seq = x_prompt[c]
        start = 0
    else:
        seq = x_sample[0]
        start = (c - 4) * NTOK
    S = seq.shape[0]
    xe = np.zeros((XE_ROWS, D), np.float32)
    lo = start - 8
    hi = start + NTOK + 8
    slo, shi = max(lo, 0), min(hi, S)
    xe[slo - lo:shi - lo] = seq[slo:shi]
    ecnt = np.zeros((4, 16), np.float32)
    for g, w in enumerate(POOL_W):
        for idx in range(16):
            t = start + (idx if idx < 8 else NTOK - 16 + idx)
            l = max(t - w // 2, 0)
            h = min(t + w - w // 2, S)
            ecnt[g, idx] = 1.0 / float(h - l)
    ecnt_r = np.ascontiguousarray(np.broadcast_to(ecnt.reshape(1, 64), (128, 64))).astype(np.float32)
    xet = np.empty((NT, 128, 8, TE), np.float32)
    for i in range(NT):
        xet[i] = xe[T * i:T * i + TE, :].reshape(TE, 8, 128).transpose(2, 1, 0)
    return xe, xet, ecnt_r


_NC_CACHE = {}


def kernel(x_prompt, x_sample, w_in, conv_w, pool_w, pool_scale, w_out, ln1_g, ln1_b,
           w_ff1, w_ff2, ln2_g, ln2_b):
    f = lambda a: np.ascontiguousarray(np.asarray(a, dtype=np.float32))
    x_prompt, x_sample = f(x_prompt), f(x_sample)
    w_in0, w_out0, w_ff10, w_ff20 = f(w_in)[0], f(w_out)[0], f(w_ff1)[0], f(w_ff2)[0]
    cw = np.ascontiguousarray(f(conv_w)[0].reshape(3, 4, 128).transpose(2, 1, 0).reshape(128, 12))
    pw = f(pool_w)[0]
    psc = np.ascontiguousarray(f(pool_scale)[0].reshape(4, 128).T)
    rep = lambda v: np.ascontiguousarray(np.broadcast_to(f(v)[0].reshape(1, D), (128, D)))
    g1r, b1r, g2r, b2r = rep(ln1_g), rep(ln1_b), rep(ln2_g), rep(ln2_b)
    col = lambda v: np.ascontiguousarray(f(v)[0].reshape(8, 128).T)
    g1c, b1c = col(ln1_g), col(ln1_b)

    if "nc" not in _NC_CACHE:
        _NC_CACHE["nc"] = build_program()
    nc = _NC_CACHE["nc"]

    in_maps = []
    for c in range(N_CORES):
        xe, xet, ecnt_r = _core_inputs(c, x_prompt, x_sample)
        in_maps.append({"xe": xe, "xet": xet, "w_in": w_in0, "w_out": w_out0, "w_ff1": w_ff10, "w_ff2": w_ff20,
                        "cw": cw, "pw": pw, "psc": psc, "ecnt": ecnt_r,
                        "g1r": g1r, "b1r": b1r, "g2r": g2r, "b2r": b2r, "g1c": g1c, "b1c": b1c})
    res = run_bass_kernel_spmd(nc, in_maps, core_ids=list(range(N_CORES)))
    outs = [np.asarray(r["y"], dtype=np.float32) for r in res.results]
    y_prompt = np.stack(outs[0:4], axis=0)
    y_sample = np.concatenate(outs[4:8], axis=0)[None]
    return (y_prompt, y_sample)
```

```python
from contextlib import ExitStack

import numpy as np
import concourse.bass as bass
import concourse.mybir as mybir
from concourse.bass_utils import run_bass_kernel_spmd

F32 = mybir.dt.float32
BF16 = mybir.dt.bfloat16
AF = mybir.ActivationFunctionType
ALU = mybir.AluOpType

D = 1024
DFF = 4096
NTOK = 4096
T = 512
NT = NTOK // T
NS = T // 128
TE = T + 16
XE_ROWS = NTOK + 16
ALPHA = 2.0 ** 0.25
EPS = 1e-5
POOL_W = (2, 4, 8, 16)
N_CORES = 8


class Buf:
    __slots__ = ("name", "w", "r", "const")

    def __init__(self, name, const=False):
        self.name = name
        self.w = None
        self.r = []
        self.const = const


class Eng:
    def __init__(self, name, sem, same_engine_sync=True):
        self.name = name
        self.sem = sem
        self.cnt = 0
        self.seen = {}
        self.ops = []
        self.same_engine_sync = same_engine_sync

    def _wait_for(self, toks):
        need = {}
        for tok in toks:
            if tok is None:
                continue
            sem, val = tok
            if sem is self.sem and not self.same_engine_sync:
                continue
            if self.seen.get(id(sem), 0) >= val:
                continue
            if need.get(id(sem), (None, 0))[1] < val:
                need[id(sem)] = (sem, val)
        for sem, val in need.values():
            self.ops.append(("wait", sem, val))
            self.seen[id(sem)] = val

    @staticmethod
    def _deps(reads, writes):
        toks = []
        for b in reads:
            toks.append(b.w)
        for b in writes:
            toks.append(b.w)
            toks.extend(b.r)
        return toks

    @staticmethod
    def _record(tok, reads, writes):
        for b in reads:
            if not b.const:
                b.r.append(tok)
        for b in writes:
            b.w = tok
            b.r = []

    def emit(self, fn, reads=(), writes=(), signal=True, excl=()):
        self._wait_for(self._deps(reads, writes))
        own = self.sem
        self._wait_for([t for t in self._deps((), excl) if t is not None and t[0] is not own])
        writes = list(writes) + list(excl)
        if signal:
            self.cnt += 1
            tok = (self.sem, self.cnt)
        else:
            tok = (self.sem, self.cnt + 1)
        self.ops.append(("ins", fn, self.sem if signal else None, 1))
        self._record(tok, reads, writes)

    def dma(self, fn, sem_state, reads=(), writes=()):
        self._wait_for(self._deps(reads, writes))
        sem_state[1] += 16
        tok = (sem_state[0], sem_state[1])
        self.ops.append(("ins", fn, sem_state[0], 16))
        self._record(tok, reads, writes)
        return tok

    def dma_multi(self, fns, sem_state, reads=(), writes=()):
        self._wait_for(self._deps(reads, writes))
        for fn in fns:
            sem_state[1] += 16
            self.ops.append(("ins", fn, sem_state[0], 16))
        tok = (sem_state[0], sem_state[1])
        self._record(tok, reads, writes)
        return tok

    def wait_tok(self, tok):
        self._wait_for([tok])

    def replay(self, h):
        for op in self.ops:
            if op[0] == "wait":
                h.wait_ge(op[1], op[2])
            else:
                ins = op[1](h)
                if op[2] is not None:
                    ins.then_inc(op[2], op[3])


def alias_barrier(new_bufs, old_bufs):
    toks = []
    for b in old_bufs:
        if b.w is not None:
            toks.append(b.w)
        toks.extend(b.r)
    for b in new_bufs:
        b.r = list(b.r) + toks


def build_program():
    nc = bass.Bass("TRN2", target_bir_lowering=False)

    def din(name, shape, dt=F32):
        return nc.dram_tensor(name, list(shape), dt, kind="ExternalInput").ap()

    xe = din("xe", [XE_ROWS, D])
    xet = din("xet", [NT, 128, 8, TE])
    w_in = din("w_in", [D, 2048])
    w_out = din("w_out", [D, D])
    w_ff1 = din("w_ff1", [D, DFF])
    w_ff2 = din("w_ff2", [DFF, D])
    cw_d = din("cw", [128, 12])
    pw_d = din("pw", [4, 128, 128])
    psc_d = din("psc", [128, 4])
    ecnt_d = din("ecnt", [128, 64])
    g1_d = din("g1r", [128, D])
    b1_d = din("b1r", [128, D])
    g1c_d = din("g1c", [128, 8])
    b1c_d = din("b1c", [128, 8])
    g2_d = din("g2r", [128, D])
    b2_d = din("b2r", [128, D])
    y = nc.dram_tensor("y", [NTOK, D], F32, kind="ExternalOutput").ap()

    scr_ff1 = nc.dram_tensor("scr_ff1", [8, 128, 8, 512], BF16).ap()
    scr_ff2 = nc.dram_tensor("scr_ff2", [8, 128, 4, 1024], BF16).ap()

    with ExitStack() as es:
        E = es.enter_context

        def sb(name, shape, dt=F32):
            return E(nc.sbuf_tensor(name, list(shape), dt))

        ident = sb("ident", [128, 128])
        cw = sb("cw_s", [128, 12])
        psc = sb("psc_s", [128, 4])
        ecnt = sb("ecnt_s", [128, 64])
        pw = sb("pw_s", [128, 4, 128], BF16)
        g1 = sb("g1_s", [128, D])
        b1 = sb("b1_s", [128, D])
        g1c = sb("g1c_s", [128, 8])
        b1c = sb("b1c_s", [128, 8])
        g2 = sb("g2_s", [128, D])
        b2 = sb("b2_s", [128, D])
        win_all = sb("win_all", [128, 8, 2048], BF16)
        wout = sb("wout", [128, 8, 1024], BF16)
        wff1 = [sb(f"wff1_{i}", [128, 8, 512], BF16) for i in range(3)]
        wff2 = [sb(f"wff2_{i}", [128, 4, 1024], BF16) for i in range(2)]
        xT = sb("xT", [128, 8, TE], BF16)
        xr = [sb(f"xr{i}", [128, D]) for i in range(2)]
        x1 = sb("x1", [128, NS, D])
        x1T = sb("x1T", [128, 8, T], BF16)
        rtmp = [sb(f"rtmp{i}", [128, T]) for i in range(2)]
        vbuf = [sb(f"vbuf{i}", [128, D]) for i in range(4)]
        st = [sb(f"st{i}", [128, 12]) for i in range(2)]
        mv = [sb(f"mv{i}", [128, 2]) for i in range(2)]
        sd = [sb(f"sd{i}", [128, 1]) for i in range(2)]
        rstd = [sb(f"rstd{i}", [128, 1]) for i in range(2)]
        tmp8 = sb("tmp8", [128, 8])
        nhalf = sb("nhalf", [128, 1])
        R = sb("R", [128, 8448])
        hT = R[:, 0:8192].bitcast(BF16).rearrange("p (f t) -> p f t", t=T)
        vS = R[:, 2112:2640]
        gcv = R[:, 2640:3168]
        acc = R[:, 3168:3696]
        pS = R[:, 3696:4224]
        Aa = R[:, 4224:4752]
        Ab = R[:, 4752:5280]
        pooled = R[:, 5280:6304].bitcast(BF16).rearrange("p (g t) -> p g t", t=T)
        ycat = R[:, 6304:8352].bitcast(BF16).rearrange("p (k t) -> p k t", t=T)
        bank = [E(nc.psum_tensor(f"bank{i}", [128, 512], F32)) for i in range(8)]

        def sem(name):
            return E(nc.semaphore(name))

        pe = Eng("pe", sem("s_pe"), same_engine_sync=False)
        act = Eng("act", sem("s_act"))
        dve = Eng("dve", sem("s_dve"))
        pool = Eng("pool", sem("s_pool"))
        sp = Eng("sp", sem("s_sp"), same_engine_sync=False)

        B_ident = Buf("ident")
        B_consts = Buf("consts")
        B_pw = Buf("pw")
        B_win = [Buf("win_lo"), Buf("win_hi")]
        B_win_lo = [Buf("win_lo_a"), Buf("win_lo_b")]
        B_stage = Buf("w_in_stage")
        B_wout = Buf("wout")
        B_wff1 = [Buf(f"wff1_{i}") for i in range(3)]
        B_wff2 = [Buf(f"wff2_{i}") for i in range(2)]
        B_xr = [Buf(f"xr{i}") for i in range(2)]
        B_x1 = [[Buf(f"x1_{s}_{n}") for n in range(2)] for s in range(NS)]
        B_x1T = [[Buf(f"x1T_{s}_{k}") for k in range(8)] for s in range(NS)]
        B_rtmp = [Buf(f"rtmp{i}") for i in range(4)]
        B_vbuf = [[Buf(f"vbuf{i}_{n}") for n in range(2)] for i in range(4)]
        B_st = [[Buf(f"st{i}_{n}") for n in range(2)] for i in range(2)]
        B_mv = [Buf(f"mv{i}") for i in range(2)]
        B_sd = [Buf(f"sd{i}") for i in range(2)]
        B_rstd = [Buf(f"rstd{i}") for i in range(2)]
        B_tmp8 = Buf("tmp8")
        B_h = [[Buf(f"h{f}_{hf}") for hf in range(2)] for f in range(32)]
        B_h_all = [b for pair in B_h for b in pair]
        B_xT = Buf("xT")
        B_vS = [Buf("vS0"), Buf("vS1")]
        B_gcv = [Buf("gcv0"), Buf("gcv1")]
        B_pS = [Buf("pS0"), Buf("pS1")]
        B_acc, B_Aa, B_Ab = Buf("acc"), Buf("Aa"), Buf("Ab")
        B_pooled = [Buf(f"pooled{g}") for g in range(4)]
        B_ycat = [Buf(f"ycat{k}") for k in range(8)]
        B_bank = [Buf(f"bank{i}") for i in range(8)]
        B_scr_ff1 = [Buf(f"scr_ff1_{g}") for g in range(8)]
        B_scr_ff2 = [Buf(f"scr_ff2_{g}") for g in range(8)]

        def mix_bufs():
            out = B_vS + B_gcv + B_pS + [B_acc, B_Aa, B_Ab] + B_pooled + B_ycat
            return out

        def dsem(name):
            return [sem(name), 0]

        S_const = dsem("d_const")
        S_pw = dsem("d_pw")
        S_scr_ff1 = [dsem(f"d_sff1_{g}") for g in range(8)]
        S_scr_ff2 = [dsem(f"d_sff2_{g}") for g in range(8)]
        S_win = [dsem("d_win_lo"), dsem("d_win_hi")]
        S_wout = dsem("d_wout")
        S_wff1 = [dsem(f"d_wff1_{i}") for i in range(3)]
        S_wff2 = [dsem(f"d_wff2_{i}") for i in range(2)]
        S_xT = dsem("d_xT")
        S_xr = [dsem(f"d_xr{i}") for i in range(2)]
        S_y = [dsem(f"d_y{i}") for i in range(4)]

        pool.emit(lambda h: h.memset(ident[:], 0.0), writes=[B_ident])
        pool.emit(lambda h: h.affine_select(out=ident[:], in_=ident[:], pattern=[[-1, 128]],
                                            compare_op=ALU.not_equal, fill=1.0, base=0,
                                            channel_multiplier=1), writes=[B_ident])
        pool.emit(lambda h: h.memset(nhalf[:], -0.5), writes=[B_ident])
        pool.dma(lambda h: h.dma_start(out=pw[:], in_=pw_d.rearrange("g c d -> c g d")), S_pw, writes=[B_pw])
        def cast_chain():
            hist = []

            def one(fn, sem_state, buf):
                if len(hist) >= 2:
                    pool.wait_tok(hist[-2])
                hist.append(pool.dma(fn, sem_state, writes=[buf]))

            pool.wait_tok((S_win[0][0], S_win[0][1]))
            one(lambda h: h.dma_start(out=wout[:], in_=w_out.rearrange("(k p) d -> p k d", p=128)), S_wout, B_wout)
            B_wout.const = True

        B_win_hi = [Buf("win_hi_a")] + [Buf(f"win_hi_b{kk}") for kk in range(4)]
        x1_all = [b for ss in range(NS) for b in B_x1[ss]]
        sp.dma(lambda h: h.dma_start(out=x1[:, 0:4, :],
                                     in_=w_in[0:512, 1024:2048].rearrange("(k p) c -> p k c", p=128)),
               S_win[1], writes=x1_all)
        for kk in range(4):
            sp.dma(lambda h, kk=kk: h.dma_start(out=vbuf[kk][:], in_=w_in[(4 + kk) * 128:(5 + kk) * 128, 1024:2048]),
                   S_y[kk], writes=B_vbuf[kk])
        act.emit(lambda h: h.activation(out=win_all[:, 0:4, 1024:2048], in_=x1[:, 0:4, :], func=AF.Copy),
                 reads=x1_all, writes=[B_win_hi[0]])
        for kk in range(4):
            act.emit(lambda h, kk=kk: h.activation(out=win_all[:, 4 + kk, 1024:2048], in_=vbuf[kk][:], func=AF.Copy),
                     reads=B_vbuf[kk], writes=[B_win_hi[1 + kk]])
        for b_ in B_win_hi:
            b_.const = True

        stage = R[:, 0:8192].rearrange("p (k c) -> p k c", c=1024)
        sp.dma(lambda h: h.dma_start(out=stage, in_=w_in[:, 0:1024].rearrange("(k p) c -> p k c", p=128)),
               S_win[0], writes=[B_stage])
        dve.emit(lambda h: h.tensor_copy(win_all[:, 0:4, 0:1024], stage[:, 0:4, :]),
                 reads=[B_stage], writes=[B_win_lo[0]])
        dve.emit(lambda h: h.tensor_copy(win_all[:, 4:8, 0:1024], stage[:, 4:8, :]),
                 reads=[B_stage], writes=[B_win_lo[1]])
        B_win_lo[0].const = True
        B_win_lo[1].const = True
        alias_barrier(mix_bufs() + B_h_all, [B_stage])

        consts = ((cw, cw_d), (psc, psc_d), (ecnt, ecnt_d), (g1c, g1c_d), (b1c, b1c_d), (g1, g1_d), (b1, b1_d),
                  (g2, g2_d), (b2, b2_d))
        sp.dma_multi([(lambda h, dst=dst, src=src: h.dma_start(out=dst[:], in_=src)) for dst, src in consts],
                     S_const, writes=[B_consts])
        B_consts.const = True
        B_ident.const = True
        B_pw.const = True

        xr_ctr = [0]
        win_ctr = [0]
        wff1_ctr = [0]
        wff2_ctr = [0]
        ln_ctr = [0]

        def load_xT(i):
            pool.dma(lambda h: h.dma_start(out=xT[:], in_=xet[i]), S_xT, writes=[B_xT])

        def load_xr(i, s):
            slot = xr_ctr[0] % 2
            xr_ctr[0] += 1
            r0 = 8 + T * i + 128 * s
            sp.dma(lambda h: h.dma_start(out=xr[slot][:], in_=xe[r0:r0 + 128, :]), S_xr[slot], writes=[B_xr[slot]])
            return slot

        first_pass = [True]

        def load_wff1(g):
            slot = wff1_ctr[0] % 3
            wff1_ctr[0] += 1
            if first_pass[0]:
                pool.dma(lambda h: h.dma_start(
                    out=wff1[slot][:], in_=w_ff1[:, g * 512:(g + 1) * 512].rearrange("(k p) c -> p k c", p=128)),
                    S_wff1[slot], writes=[B_wff1[slot]])
                sp.dma(lambda h: h.dma_start(out=scr_ff1[g], in_=wff1[slot][:]), S_scr_ff1[g],
                       reads=[B_wff1[slot]], writes=[B_scr_ff1[g]])
            else:
                assert B_scr_ff1[g].w is not None, "w_ff1 scratch group read before it was written"
                sp.dma(lambda h: h.dma_start(out=wff1[slot][:], in_=scr_ff1[g]), S_wff1[slot],
                       reads=[B_scr_ff1[g]], writes=[B_wff1[slot]])
            return slot

        def load_wff2(g):
            slot = wff2_ctr[0] % 2
            wff2_ctr[0] += 1
            if first_pass[0]:
                pool.dma(lambda h: h.dma_start(
                    out=wff2[slot][:], in_=w_ff2[g * 512:(g + 1) * 512, :].rearrange("(f p) d -> p f d", p=128)),
                    S_wff2[slot], writes=[B_wff2[slot]])
                sp.dma(lambda h: h.dma_start(out=scr_ff2[g], in_=wff2[slot][:]), S_scr_ff2[g],
                       reads=[B_wff2[slot]], writes=[B_scr_ff2[g]])
            else:
                assert B_scr_ff2[g].w is not None, "w_ff2 scratch group read before it was written"
                sp.dma(lambda h: h.dma_start(out=wff2[slot][:], in_=scr_ff2[g]), S_wff2[slot],
                       reads=[B_scr_ff2[g]], writes=[B_wff2[slot]])
            return slot

        def load_wff2_t0(g, slot3):
            wt, wb, ws = slot3
            pool.dma(lambda h: h.dma_start(
                out=wt, in_=w_ff2[g * 512:(g + 1) * 512, :].rearrange("(f p) d -> p f d", p=128)), ws, writes=[wb])
            sp.dma(lambda h: h.dma_start(out=scr_ff2[g], in_=wt), S_scr_ff2[g], reads=[wb], writes=[B_scr_ff2[g]])

        def ln_evac(src_banks, resid_ap, resid_bufs, v, vbufs):
            for n in range(2):
                bi = src_banks[n]
                dve.emit(lambda h, n=n, bi=bi: h.scalar_tensor_tensor(
                    out=v[:, n * 512:(n + 1) * 512], in0=resid_ap[:, n * 512:(n + 1) * 512], scalar=ALPHA,
                    in1=bank[bi][:], op0=ALU.mult, op1=ALU.add),
                    reads=resid_bufs, writes=[vbufs[n]], excl=[B_bank[bi]])

        def ln_stats(v, vbufs, use_pool=True):
            p = ln_ctr[0] % 2
            ln_ctr[0] += 1
            for n in range(2):
                dve.emit(lambda h, n=n: h.bn_stats(out=st[p][:, n * 6:(n + 1) * 6], in_=v[:, n * 512:(n + 1) * 512]),
                         reads=[vbufs[n]], writes=[B_st[p][n]])
            dve.emit(lambda h: h.bn_aggr(out=mv[p][:], in_=st[p][:]), reads=B_st[p], writes=[B_mv[p]])
            if not use_pool:
                act.emit(lambda h: h.activation(out=sd[p][:], in_=mv[p][:, 1:2], func=AF.Sqrt, bias=EPS, scale=1.0),
                         reads=[B_mv[p]], writes=[B_sd[p]])
                dve.emit(lambda h: h.reciprocal(out=rstd[p][:], in_=sd[p][:]), reads=[B_sd[p]], writes=[B_rstd[p]])
                return p
            pool.emit(lambda h: h.tensor_scalar(out=sd[p][:], in0=mv[p][:, 1:2], scalar1=EPS, scalar2=None,
                                                op0=ALU.add),
                      reads=[B_mv[p]], writes=[B_sd[p]])
            pool.emit(lambda h: h.tensor_tensor(out=rstd[p][:], in0=sd[p][:], in1=nhalf[:], op=ALU.pow),
                      reads=[B_sd[p], B_ident], writes=[B_rstd[p]])
            return p

        def ln_norm(v, vbufs, p):
            dve.emit(lambda h: h.tensor_scalar(out=v, in0=v, scalar1=mv[p][:, 0:1], scalar2=rstd[p][:, 0:1],
                                               op0=ALU.subtract, op1=ALU.mult),
                     reads=[B_mv[p], B_rstd[p]], writes=vbufs)

        def ln_affine(v, vbufs, p, gam, bet):
            dve.emit(lambda h: h.scalar_tensor_tensor(out=v, in0=v, scalar=mv[p][:, 0:1], in1=gam[:],
                                                      op0=ALU.subtract, op1=ALU.mult),
                     reads=[B_mv[p], B_consts], writes=vbufs)
            dve.emit(lambda h: h.scalar_tensor_tensor(out=v, in0=v, scalar=rstd[p][:, 0:1], in1=bet[:],
                                                      op0=ALU.mult, op1=ALU.add),
                     reads=[B_rstd[p], B_consts], writes=vbufs)

        load_xT(0)
        cast_chain()
        wff1_slots = {}
        wff2_slots = {}
        store_toks = []
        deferred_ln2 = []

        for i in range(NT):
            first_pass[0] = (i == 0)
            xT_all = [B_xT]

            xr_slots = {0: load_xr(i, 0), 1: load_xr(i, 1)}
            for g in range(3):
                wff1_slots[("A", g)] = load_wff1(g)

            main_rr = [0]
            ext_rr = [0]

            HALF = TE // 2

            def proj_chunk(cb, jj, with_ext, main_lo):
                lhs = lambda k: win_all[:, k, cb * 512 + jj * 128:cb * 512 + (jj + 1) * 128]
                if not with_ext:
                    mb = main_rr[0] % 7
                    main_rr[0] += 1
                    for k in range(8):
                        pe.emit(lambda h, k=k: h.matmul(
                            bank[mb][:], lhsT=lhs(k), rhs=xT[:, k, main_lo:main_lo + 512],
                            start=(k == 0), stop=(k == 7)),
                            reads=(B_win_lo if cb < 2 else B_win_hi) + xT_all, writes=[B_bank[mb]], signal=(k == 7))
                    return mb, None
                b0 = main_rr[0] % 7
                b1 = (main_rr[0] + 1) % 7
                main_rr[0] += 2
                for k in range(8):
                    for bb, c0 in ((b0, 0), (b1, HALF)):
                        pe.emit(lambda h, k=k, bb=bb, c0=c0: h.matmul(
                            bank[bb][:, 0:HALF], lhsT=lhs(k), rhs=xT[:, k, c0:c0 + HALF],
                            start=(k == 0), stop=(k == 7)),
                            reads=(B_win_lo if cb < 2 else B_win_hi) + xT_all, writes=[B_bank[bb]], signal=(k == 7 and bb == b1))
                return b0, b1

            pending_poolmix = []

            def do_poolmix(g):
                pe.emit(lambda h: h.matmul(bank[7][:], lhsT=pw[:, g, :], rhs=pooled[:, g, :], start=True, stop=True),
                        reads=[B_pw, B_pooled[g]], writes=[B_bank[7]])
                act.emit(lambda h: h.activation(out=ycat[:, 4 + g, :], in_=bank[7][:], func=AF.Identity,
                                                scale=psc[:, g:g + 1]),
                         reads=[B_consts], writes=[B_ycat[4 + g]], excl=[B_bank[7]])

            sum_eng = dve if i == 0 else pool
            for j in range(4):
                pj = 3 - j
                mb_v, eb_v = proj_chunk(2, j, True, 0)
                act.emit(lambda h, mb=mb_v: h.activation(out=vS[:, 0:HALF], in_=bank[mb][:, 0:HALF], func=AF.Copy),
                         writes=[B_vS[0]], excl=[B_bank[mb_v]])
                act.emit(lambda h, eb=eb_v: h.activation(out=vS[:, HALF:TE], in_=bank[eb][:, 0:HALF], func=AF.Copy),
                         writes=[B_vS[1]], excl=[B_bank[eb_v]])
                mb_c, eb_c = proj_chunk(1, j, True, 0)
                dve.emit(lambda h, mb=mb_c: h.tensor_tensor(out=gcv[:, 0:HALF], in0=bank[mb][:, 0:HALF],
                                                            in1=vS[:, 0:HALF], op=ALU.mult),
                         reads=[B_vS[0]], writes=[B_gcv[0]], excl=[B_bank[mb_c]])
                dve.emit(lambda h, eb=eb_c: h.tensor_tensor(out=gcv[:, HALF:TE], in0=bank[eb][:, 0:HALF],
                                                            in1=vS[:, HALF:TE], op=ALU.mult),
                         reads=[B_vS[1]], writes=[B_gcv[1]], excl=[B_bank[eb_c]])
                act.emit(lambda h, j=j: h.activation(out=acc[:, 0:512], in_=gcv[:, 8:520], func=AF.Identity,
                                                     scale=cw[:, 3 * j + 1:3 * j + 2]),
                         reads=B_gcv + [B_consts], writes=[B_acc])
                dve.emit(lambda h, j=j: h.scalar_tensor_tensor(out=acc[:, 0:512], in0=gcv[:, 7:519],
                                                               scalar=cw[:, 3 * j:3 * j + 1], in1=acc[:, 0:512],
                                                               op0=ALU.mult, op1=ALU.add),
                         reads=B_gcv + [B_consts], writes=[B_acc])
                dve.emit(lambda h, j=j: h.scalar_tensor_tensor(out=acc[:, 0:512], in0=gcv[:, 9:521],
                                                               scalar=cw[:, 3 * j + 2:3 * j + 3], in1=acc[:, 0:512],
                                                               op0=ALU.mult, op1=ALU.add),
                         reads=B_gcv + [B_consts], writes=[B_acc])
                mb_b, _ = proj_chunk(0, j, False, 8)
                dve.emit(lambda h, mb=mb_b, j=j: h.tensor_tensor(out=ycat[:, j, :], in0=bank[mb][:], in1=acc[:, 0:512],
                                                                op=ALU.mult),
                         reads=[B_acc], writes=[B_ycat[j]], excl=[B_bank[mb_b]])
                mb_p, eb_p = proj_chunk(3, pj, True, 0)
                act.emit(lambda h, mb=mb_p: h.activation(out=pS[:, 0:HALF], in_=bank[mb][:, 0:HALF], func=AF.Copy),
                         writes=[B_pS[0]], excl=[B_bank[mb_p]])
                act.emit(lambda h, eb=eb_p: h.activation(out=pS[:, HALF:TE], in_=bank[eb][:, 0:HALF], func=AF.Copy),
                         writes=[B_pS[1]], excl=[B_bank[eb_p]])
                w = POOL_W[pj]
                cur, cur_b, L = pS, B_pS, TE
                pp = [(Aa, [B_Aa]), (Ab, [B_Ab])]
                m = 1
                step = 0
                while m < w:
                    dst, dst_b = pp[step % 2]
                    L2 = L - m
                    sum_eng.emit(lambda h, dst=dst, cur=cur, L2=L2, m=m: h.tensor_tensor(
                        out=dst[:, 0:L2], in0=cur[:, 0:L2], in1=cur[:, m:m + L2], op=ALU.add),
                        reads=cur_b, writes=dst_b)
                    cur, cur_b, L = dst, dst_b, L2
                    m *= 2
                    step += 1
                off = 8 - w // 2
                dve.emit(lambda h, cur=cur, off=off, pj=pj, w=w: h.scalar_tensor_tensor(
                    out=pooled[:, pj, :], in0=cur[:, off:off + 512], scalar=1.0 / w, in1=pS[:, 8:520],
                    op0=ALU.mult, op1=ALU.subtract),
                    reads=cur_b + B_pS, writes=[B_pooled[pj]])
                if i == 0 or i == NT - 1:
                    q0 = 0 if i == 0 else 504
                    e0 = 16 * pj + (0 if i == 0 else 8)
                    dve.emit(lambda h, cur=cur, off=off, q0=q0, e0=e0: h.tensor_tensor(
                        out=tmp8[:], in0=cur[:, off + q0:off + q0 + 8], in1=ecnt[:, e0:e0 + 8], op=ALU.mult),
                        reads=cur_b + [B_consts], writes=[B_tmp8])
                    dve.emit(lambda h, q0=q0, pj=pj: h.tensor_tensor(
                        out=pooled[:, pj, q0:q0 + 8], in0=tmp8[:], in1=pS[:, 8 + q0:16 + q0], op=ALU.subtract),
                        reads=[B_tmp8] + B_pS, writes=[B_pooled[pj]])
                if pending_poolmix:
                    do_poolmix(pending_poolmix.pop(0))
                pending_poolmix.append(pj)
            while len(pending_poolmix) > 1:
                do_poolmix(pending_poolmix.pop(0))
            if i + 1 < NT:
                load_xT(i + 1)

            if i > 0:
                for g in range(2):
                    wff2_slots[g] = load_wff2(g)

            korder = [0, 7, 1, 6, 2, 5, 3, 4]

            def wo_mm(s, n, idx):
                k = korder[idx]
                bi = 2 * s + n
                pe.emit(lambda h: h.matmul(
                    bank[bi][:], lhsT=ycat[:, k, 128 * s:128 * s + 128],
                    rhs=wout[:, k, 512 * n:512 * n + 512], start=(idx == 0), stop=(idx == 7)),
                    reads=[B_ycat[k], B_wout], writes=[B_bank[bi]], signal=(idx == 7))

            for idx in range(7):
                for n in range(2):
                    wo_mm(0, n, idx)
            do_poolmix(pending_poolmix.pop(0))
            for n in range(2):
                wo_mm(0, n, 7)
            for idx in range(8):
                for n in range(2):
                    wo_mm(1, n, idx)

            def wo_pair1(idx_range):
                for idx in idx_range:
                    for s in (2, 3):
                        for n in range(2):
                            wo_mm(s, n, idx)

            def ln1(s):
                slot = xr_slots.pop(s)
                v = x1[:, s, :]
                ln_evac((2 * s, 2 * s + 1), xr[slot], [B_xr[slot]], v, B_x1[s])
                if s + 2 < NS:
                    xr_slots[s + 2] = load_xr(i, s + 2)
                p = ln_stats(v, B_x1[s], use_pool=(i > 0))
                ln_norm(v, B_x1[s], p)

            def tr1(s):
                for hh in range(2):
                    bi = (2 * s + hh) % 8
                    for kk in range(4):
                        k = 4 * hh + kk
                        pe.emit(lambda h, bi=bi, kk=kk, k=k: h.transpose(
                            bank[bi][:, kk * 128:(kk + 1) * 128], x1[:, s, k * 128:(k + 1) * 128], ident[:]),
                            reads=B_x1[s] + [B_ident], writes=[B_bank[bi]], signal=(kk == 3))
                    for kk in range(4):
                        k = 4 * hh + kk
                        if hh == 0:
                            act.emit(lambda h, bi=bi, kk=kk, k=k: h.activation(
                                out=x1T[:, k, 128 * s:128 * s + 128], in_=bank[bi][:, kk * 128:(kk + 1) * 128],
                                func=AF.Identity, scale=g1c[:, k:k + 1], bias=b1c[:, k:k + 1]),
                                reads=[B_consts], writes=[B_x1T[s][k]], excl=[B_bank[bi]])
                        else:
                            dve.emit(lambda h, bi=bi, kk=kk, k=k: h.tensor_scalar(
                                out=x1T[:, k, 128 * s:128 * s + 128], in0=bank[bi][:, kk * 128:(kk + 1) * 128],
                                scalar1=g1c[:, k:k + 1], scalar2=b1c[:, k:k + 1], op0=ALU.mult, op1=ALU.add),
                                reads=[B_consts], writes=[B_x1T[s][k]], excl=[B_bank[bi]])

            def ff1_group(hf, g, slot):
                x1T_half = [B_x1T[s][k] for s in (2 * hf, 2 * hf + 1) for k in range(8)]
                c0 = 256 * hf
                for fl in range(4):
                    f = 4 * g + fl
                    bi = f % (4 if hf == 0 else 8)
                    rp = f % 4
                    rt = rtmp[rp // 2][:, 256 * (rp % 2):256 * (rp % 2) + 256]
                    for k in range(8):
                        pe.emit(lambda h, bi=bi, k=k, fl=fl: h.matmul(
                            bank[bi][:, 0:256], lhsT=wff1[slot][:, k, fl * 128:(fl + 1) * 128],
                            rhs=x1T[:, k, c0:c0 + 256], start=(k == 0), stop=(k == 7)),
                            reads=[B_wff1[slot]] + x1T_half, writes=[B_bank[bi]], signal=(k == 7))
                    act.emit(lambda h, bi=bi, rt=rt: h.activation(out=rt, in_=bank[bi][:, 0:256], func=AF.Relu),
                             writes=[B_rtmp[rp]], excl=[B_bank[bi]])
                    dve.emit(lambda h, rt=rt, f=f: h.tensor_tensor(out=hT[:, f, c0:c0 + 256], in0=rt, in1=rt,
                                                                   op=ALU.mult),
                             reads=[B_rtmp[rp]], writes=[B_h[f][hf]])
                    if hf == 0 and f >= 12 and deferred_ln2:
                        deferred_ln2.pop(0)()

            ln1(0)
            ln1(1)
            wo_pair1(range(0, 2))
            tr1(0)
            wo_pair1(range(2, 6))
            tr1(1)
            wo_pair1(range(6, 8))
            alias_barrier(B_h_all, mix_bufs())
            ln1(2)
            ln1(3)
            if i > 0:
                first = [wff1_slots.pop(("A", g)) for g in range(3)]
                ff1_group(0, 0, first[0])
                ff1_group(0, 1, first[1])
                tr1(2)
                tr1(3)
                ff1_group(0, 2, first[2])
                for g in range(3):
                    ff1_group(1, g, first[g])
                    wff1_slots[("A", g + 3)] = load_wff1(g + 3)
                for g in range(3, 8):
                    slot = wff1_slots.pop(("A", g))
                    ff1_group(0, g, slot)
                    ff1_group(1, g, slot)
                    if g + 3 < 8:
                        wff1_slots[("A", g + 3)] = load_wff1(g + 3)
                while deferred_ln2:
                    deferred_ln2.pop(0)()
            else:
                for g in range(2):
                    wff2_slots[g] = load_wff2(g)
                tr1(2)
                tr1(3)
                for g in range(8):
                    slot = wff1_slots.pop(("A", g))
                    ff1_group(0, g, slot)
                    ff1_group(1, g, slot)
                    if g + 3 < 8:
                        wff1_slots[("A", g + 3)] = load_wff1(g + 3)
                t0_ring = {0: (wff2[0][:], B_wff2[0], S_wff2[0]), 1: (wff2[1][:], B_wff2[1], S_wff2[1])}
                for g, k1 in ((2, 2), (3, 0), (4, 1)):
                    view = wff1[k1][:].rearrange("p k c -> p (k c)").rearrange("p (f d) -> p f d", d=1024)
                    t0_ring[g] = (view, B_wff1[k1], S_wff1[k1])
                    load_wff2_t0(g, t0_ring[g])
            def x1_affine(s):
                extra = ([b for ss in range(NS) for b in B_x1[ss]] + [B_h[31][1]]) if s == 0 else []
                pool.emit(lambda h: h.tensor_tensor(out=x1[:, s, :], in0=x1[:, s, :], in1=g1[:], op=ALU.mult),
                          reads=[B_consts] + extra, writes=B_x1[s])
                pool.emit(lambda h: h.tensor_tensor(out=x1[:, s, :], in0=x1[:, s, :], in1=b1[:], op=ALU.add),
                          reads=[B_consts], writes=B_x1[s])

            if i > 0:
                for s in range(NS):
                    x1_affine(s)
            def ff2_round(subtiles, last_round, n_groups=8):
                for g in range(n_groups):
                    slot = wff2_slots.pop(g)
                    if g == 7:
                        order = [(fl, s) for s in subtiles for fl in range(4)]
                    else:
                        order = [(fl, s) for fl in range(4) for s in subtiles]
                    for idx, (fl, s) in enumerate(order):
                        f = 4 * g + fl
                        for n in range(2):
                            bi = 2 * s + n
                            pe.emit(lambda h, bi=bi, f=f, fl=fl, s=s, n=n, slot=slot: h.matmul(
                                bank[bi][:], lhsT=hT[:, f, 128 * s:128 * s + 128],
                                rhs=wff2[slot][:, fl, 512 * n:512 * n + 512], start=(f == 0), stop=(f == 31)),
                                reads=[B_h[f][s // 2], B_wff2[slot]], writes=[B_bank[bi]],
                                signal=(f == 31 or (idx == len(order) - 1 and n == 1)))
                    if g + 2 < 8:
                        wff2_slots[g + 2] = load_wff2(g + 2)
                    elif not last_round:
                        wff2_slots[g + 2 - 8] = load_wff2(g + 2 - 8)
                    if i == 0 and g < NS:
                        x1_affine(g)

            def ln2_pieces(s, i):
                st_ = {}

                def p1():
                    st_["p"] = ln_stats(vbuf[s][:], B_vbuf[s])

                def p2():
                    p = st_["p"]
                    dve.emit(lambda h: h.scalar_tensor_tensor(out=vbuf[s][:], in0=vbuf[s][:], scalar=mv[p][:, 0:1],
                                                              in1=g2[:], op0=ALU.subtract, op1=ALU.mult),
                             reads=[B_mv[p], B_consts], writes=B_vbuf[s])

                def p3():
                    p = st_["p"]
                    dve.emit(lambda h: h.scalar_tensor_tensor(out=vbuf[s][:], in0=vbuf[s][:], scalar=rstd[p][:, 0:1],
                                                              in1=b2[:], op0=ALU.mult, op1=ALU.add),
                             reads=[B_rstd[p], B_consts], writes=B_vbuf[s])
                    r0 = T * i + 128 * s
                    tok = sp.dma(lambda h: h.dma_start(out=y[r0:r0 + 128, :], in_=vbuf[s][:]), S_y[s],
                                 reads=B_vbuf[s])
                    if i == NT - 1:
                        store_toks.append(tok)
                return [p1, p2, p3]

            if i == 0:
                wff2_slots.pop(0)
                wff2_slots.pop(1)
                for g in range(8):
                    wt, wb, ws = t0_ring[g]
                    for fl in range(4):
                        f = 4 * g + fl
                        for s in range(NS):
                            for n in range(2):
                                bi = 2 * s + n
                                pe.emit(lambda h, bi=bi, f=f, fl=fl, s=s, n=n, wt=wt: h.matmul(
                                    bank[bi][:], lhsT=hT[:, f, 128 * s:128 * s + 128],
                                    rhs=wt[:, fl, 512 * n:512 * n + 512], start=(f == 0), stop=(f == 31)),
                                    reads=[B_h[f][s // 2], wb], writes=[B_bank[bi]],
                                    signal=(f == 31 or (fl == 3 and s == NS - 1 and n == 1)))
                    if g + 5 < 8:
                        t0_ring[g + 5] = t0_ring[g]
                        load_wff2_t0(g + 5, t0_ring[g])
                    if g < NS:
                        x1_affine(g)
                alias_barrier(mix_bufs(), B_h_all)
                for s in range(NS):
                    ln_evac((2 * s, 2 * s + 1), x1[:, s, :], B_x1[s], vbuf[s][:], B_vbuf[s])
                for s in range(NS):
                    deferred_ln2.extend(ln2_pieces(s, i))
            elif i < NT - 1:
                ff2_round(list(range(NS)), True)
                alias_barrier(mix_bufs(), B_h_all)
                for s in range(NS):
                    ln_evac((2 * s, 2 * s + 1), x1[:, s, :], B_x1[s], vbuf[s][:], B_vbuf[s])
                for s in range(NS):
                    deferred_ln2.extend(ln2_pieces(s, i))
            else:
                tail_w = {}
                for g, k1 in ((6, 0), (7, 1)):
                    view = wff1[k1][:].rearrange("p k c -> p (k c)").rearrange("p (f d) -> p f d", d=1024)
                    sp.dma(lambda h, g=g, view=view: h.dma_start(out=view, in_=scr_ff2[g]), S_wff1[k1],
                           reads=[B_scr_ff2[g]], writes=[B_wff1[k1]])
                    tail_w[g] = (view, B_wff1[k1])
                ff2_round(list(range(NS)), True, n_groups=4)
                for g in (4, 5):
                    sl = wff2_slots.pop(g)
                    tail_w[g] = (wff2[sl][:], B_wff2[sl])
                for s in range(NS):
                    for f in range(16, 32):
                        wt, wb = tail_w[f // 4]
                        for n in range(2):
                            bi = 2 * s + n
                            pe.emit(lambda h, bi=bi, f=f, s=s, n=n, wt=wt: h.matmul(
                                bank[bi][:], lhsT=hT[:, f, 128 * s:128 * s + 128],
                                rhs=wt[:, f % 4, 512 * n:512 * n + 512], start=False, stop=(f == 31)),
                                reads=[B_h[f][s // 2], wb], writes=[B_bank[bi]], signal=(f == 31))
                    ln_evac((2 * s, 2 * s + 1), x1[:, s, :], B_x1[s], vbuf[s][:], B_vbuf[s])
                    for piece in ln2_pieces(s, i):
                        piece()

        for tok in store_toks:
            sp.wait_tok(tok)

        with nc.Block() as block:
            @block.tensor
            def _(h):
                pe.replay(h)

            @block.scalar
            def _(h):
                act.replay(h)

            @block.vector
            def _(h):
                dve.replay(h)

            @block.gpsimd
            def _(h):
                pool.replay(h)

            @block.sync
            def _(h):
                sp.replay(h)
    return nc


def _core_inputs(c, x_prompt, x_sample):
    if c < 4:
        seq = x_prompt[c]
        start = 0
    else:
        seq = x_sample[0]
        start = (c - 4) * NTOK
    S = seq.shape[0]
    xe = np.zeros((XE_ROWS, D), np.float32)
    lo = start - 8
    hi = start + NTOK + 8
    slo, shi = max(lo, 0), min(hi, S)
    xe[slo - lo:shi - lo] = seq[slo:shi]
    ecnt = np.zeros((4, 16), np.float32)
    for g, w in enumerate(POOL_W):
        for idx in range(16):
            t = start + (idx if idx < 8 else NTOK - 16 + idx)
            l = max(t - w // 2, 0)
            h = min(t + w - w // 2, S)
            ecnt[g, idx] = 1.0 / float(h - l)
    ecnt_r = np.ascontiguousarray(np.broadcast_to(ecnt.reshape(1, 64), (128, 64))).astype(np.float32)
    xet = np.empty((NT, 128, 8, TE), np.float32)
    for i in range(NT):
        xet[i] = xe[T * i:T * i + TE, :].reshape(TE, 8, 128).transpose(2, 1, 0)
    return xe, xet, ecnt_r


_NC_CACHE = {}


def kernel(x_prompt, x_sample, w_in, conv_w, pool_w, pool_scale, w_out, ln1_g, ln1_b,
           w_ff1, w_ff2, ln2_g, ln2_b):
    f = lambda a: np.ascontiguousarray(np.asarray(a, dtype=np.float32))
    x_prompt, x_sample = f(x_prompt), f(x_sample)
    w_in0, w_out0, w_ff10, w_ff20 = f(w_in)[0], f(w_out)[0], f(w_ff1)[0], f(w_ff2)[0]
    cw = np.ascontiguousarray(f(conv_w)[0].reshape(3, 4, 128).transpose(2, 1, 0).reshape(128, 12))
    pw = f(pool_w)[0]
    psc = np.ascontiguousarray(f(pool_scale)[0].reshape(4, 128).T)
    rep = lambda v: np.ascontiguousarray(np.broadcast_to(f(v)[0].reshape(1, D), (128, D)))
    g1r, b1r, g2r, b2r = rep(ln1_g), rep(ln1_b), rep(ln2_g), rep(ln2_b)
    col = lambda v: np.ascontiguousarray(f(v)[0].reshape(8, 128).T)
    g1c, b1c = col(ln1_g), col(ln1_b)

    if "nc" not in _NC_CACHE:
        _NC_CACHE["nc"] = build_program()
    nc = _NC_CACHE["nc"]

    in_maps = []
    for c in range(N_CORES):
        xe, xet, ecnt_r = _core_inputs(c, x_prompt, x_sample)
        in_maps.append({"xe": xe, "xet": xet, "w_in": w_in0, "w_out": w_out0, "w_ff1": w_ff10, "w_ff2": w_ff20,
                        "cw": cw, "pw": pw, "psc": psc, "ecnt": ecnt_r,
                        "g1r": g1r, "b1r": b1r, "g2r": g2r, "b2r": b2r, "g1c": g1c, "b1c": b1c})
    res = run_bass_kernel_spmd(nc, in_maps, core_ids=list(range(N_CORES)))
    outs = [np.asarray(r["y"], dtype=np.float32) for r in res.results]
    y_prompt = np.stack(outs[0:4], axis=0)
    y_sample = np.concatenate(outs[4:8], axis=0)[None]
    return (y_prompt, y_sample)
```
